# Optimizing a Trainium2 kernel written in Bass

```python
import math
import jax
import jax.numpy as jnp
from jax import lax
import numpy as np

D_MODEL = 1024
BATCH = 32
SEQ = 2048
DEPTH = 4

N_A_LAYERS = max(1, DEPTH // 2)
N_B_LAYERS = DEPTH - N_A_LAYERS

SSM_GROUP = 16
SSM_GROUPS = D_MODEL // SSM_GROUP
SSM_STATE = 64
STEP_MIN = 1e-3
STEP_MAX = 1e-1

N_HEADS = 16
QK_NOPE_DIM = 64
QK_ROPE_DIM = 32
QK_DIM = QK_NOPE_DIM + QK_ROPE_DIM
V_DIM = 64
Q_LORA_RANK = 384
KV_LORA_RANK = 256
ROPE_THETA = 10000.0
Q_BLOCK = 128

D_FF = -(-8 * D_MODEL // (3 * 256)) * 256

EPS = 1e-6
NEG_INF = -1e30

kernel_name = 'yoco_s5_mla_hybrid'


def rms_norm(x, gain):
    xf = x.astype(jnp.float32)
    y = xf * lax.rsqrt(jnp.mean(xf * xf, axis=-1, keepdims=True) + EPS)
    return (y * gain.astype(jnp.float32)).astype(x.dtype)


def swiglu_ffn(h, w_gate_up, w_down):
    gate, up = jnp.split(h @ w_gate_up, 2, axis=-1)
    return (jax.nn.silu(gate) * up) @ w_down


def cmul(ar, ai, br, bi):
    return ar * br - ai * bi, ar * bi + ai * br


def s5_mixer(h, w_in, lam_re, lam_im, log_step, b_re, b_im, c_re, c_im, d_skip, w_glu):
    bsz, seq, _ = h.shape
    f32 = jnp.float32
    u = (h @ w_in).reshape(bsz, seq, SSM_GROUPS, SSM_GROUP).astype(f32)
    step = jnp.exp(log_step.astype(f32))[:, None]
    lr, li = lam_re.astype(f32), lam_im.astype(f32)
    decay = jnp.exp(lr * step)
    a_re, a_im = decay * jnp.cos(li * step), decay * jnp.sin(li * step)
    den = lr * lr + li * li
    f_re = ((a_re - 1.0) * lr + a_im * li) / den
    f_im = (a_im * lr - (a_re - 1.0) * li) / den
    br, bi = cmul(f_re[..., None], f_im[..., None], b_re.astype(f32), b_im.astype(f32))
    cr, ci = c_re.astype(f32), c_im.astype(f32)

    def combine(e_i, e_j):
        ar_i, ai_i, xr_i, xi_i = e_i
        ar_j, ai_j, xr_j, xi_j = e_j
        ar, ai = cmul(ar_j, ai_j, ar_i, ai_i)
        xr, xi = cmul(ar_j, ai_j, xr_i, xi_i)
        return ar, ai, xr + xr_j, xi + xi_j

    def scan_sequence(u_seq):
        bu_re = jnp.einsum('gnk,lgk->lgn', br, u_seq)
        bu_im = jnp.einsum('gnk,lgk->lgn', bi, u_seq)
        ar_seq = jnp.broadcast_to(a_re, bu_re.shape)
        ai_seq = jnp.broadcast_to(a_im, bu_im.shape)
        _, _, xr, xi = lax.associative_scan(combine, (ar_seq, ai_seq, bu_re, bu_im), axis=0)
        return jnp.einsum('gkn,lgn->lgk', cr, xr) - jnp.einsum('gkn,lgn->lgk', ci, xi)

    y = lax.map(scan_sequence, u)
    y = y + d_skip.astype(f32).reshape(SSM_GROUPS, SSM_GROUP) * u
    z = jax.nn.gelu(y.reshape(bsz, seq, D_MODEL)).astype(h.dtype)
    val, gate = jnp.split(z @ w_glu, 2, axis=-1)
    return val * jax.nn.sigmoid(gate)


def rope_tables(positions):
    inv_freq = ROPE_THETA ** (-jnp.arange(0, QK_ROPE_DIM, 2, dtype=jnp.float32) / QK_ROPE_DIM)
    ang = positions.astype(jnp.float32)[..., None] * inv_freq
    return jnp.cos(ang), jnp.sin(ang)


def apply_rope(x, cos, sin):
    x1, x2 = jnp.split(x.astype(jnp.float32), 2, axis=-1)
    return jnp.concatenate([x1 * cos - x2 * sin, x1 * sin + x2 * cos], axis=-1).astype(x.dtype)


def mla_shared_kv(h, w_kv_a, kv_a_norm, w_kv_b, k_nope_norm, k_rope_norm, cos, sin):
    bsz, seq, _ = h.shape
    ckv = h @ w_kv_a
    c_kv = rms_norm(ckv[..., :KV_LORA_RANK], kv_a_norm)
    k_rope = apply_rope(rms_norm(ckv[..., KV_LORA_RANK:], k_rope_norm), cos, sin)
    kv = (c_kv @ w_kv_b).reshape(bsz, seq, N_HEADS, QK_NOPE_DIM + V_DIM)
    k_nope = rms_norm(kv[..., :QK_NOPE_DIM], k_nope_norm)
    v = kv[..., QK_NOPE_DIM:]
    k_rope = jnp.broadcast_to(k_rope[:, :, None, :], (bsz, seq, N_HEADS, QK_ROPE_DIM))
    k = jnp.concatenate([k_nope, k_rope], axis=-1)
    return k.transpose(0, 2, 1, 3), v.transpose(0, 2, 1, 3)


def mla_queries(h, w_q_a, q_a_norm, w_q_b, q_nope_norm, q_rope_norm, cos, sin):
    bsz, seq, _ = h.shape
    c_q = rms_norm(h @ w_q_a, q_a_norm)
    q = (c_q @ w_q_b).reshape(bsz, seq, N_HEADS, QK_DIM)
    q_nope = rms_norm(q[..., :QK_NOPE_DIM], q_nope_norm)
    q_rope = apply_rope(rms_norm(q[..., QK_NOPE_DIM:], q_rope_norm),
                        cos[:, :, None, :], sin[:, :, None, :])
    return jnp.concatenate([q_nope, q_rope], axis=-1).transpose(0, 2, 1, 3)


def causal_block_attention(q, k, v):
    bsz, n_heads, seq, dk = q.shape
    dv = v.shape[-1]
    n_blocks = seq // Q_BLOCK
    q_blocks = q.reshape(bsz, n_heads, n_blocks, Q_BLOCK, dk).transpose(2, 0, 1, 3, 4)
    scale = 1.0 / math.sqrt(dk)
    key_pos = jnp.arange(seq)

    def one_block(args):
        q_blk, blk_idx = args
        s = jnp.einsum('bhqd,bhkd->bhqk', q_blk, k).astype(jnp.float32) * scale
        query_pos = blk_idx * Q_BLOCK + jnp.arange(Q_BLOCK)
        s = jnp.where(key_pos[None, :] <= query_pos[:, None], s, NEG_INF)
        p = jax.nn.softmax(s, axis=-1).astype(v.dtype)
        return jnp.einsum('bhqk,bhkd->bhqd', p, v)

    o = lax.map(one_block, (q_blocks, jnp.arange(n_blocks)))
    return o.transpose(1, 2, 0, 3, 4).reshape(bsz, n_heads, seq, dv)


def setup_inputs(seed: int = 0) -> dict:
    key = jax.random.key(seed)
    ks = jax.random.split(key, 32)
    f32 = jnp.float32

    def dense(k, shape, fan_in):
        return jax.random.normal(k, shape, f32) * fan_in ** -0.5

    def gain(k, shape):
        return 1.0 + 0.02 * jax.random.normal(k, shape, f32)

    na, nb = N_A_LAYERS, N_B_LAYERS
    g, n, kk = SSM_GROUPS, SSM_STATE, SSM_GROUP
    x = jax.random.normal(ks[0], (BATCH, SEQ, D_MODEL), f32)
    positions = (jnp.arange(SEQ, dtype=jnp.int32)[None, :]
                 + jax.random.randint(ks[1], (BATCH, 1), 0, SEQ, dtype=jnp.int32))
    mix_norm = gain(ks[2], (DEPTH, D_MODEL))
    ffn_norm = gain(ks[3], (DEPTH, D_MODEL))
    ffn_w_gate_up = dense(ks[4], (DEPTH, D_MODEL, 2 * D_FF), D_MODEL)
    ffn_w_down = dense(ks[5], (DEPTH, D_FF, D_MODEL), D_FF)
    ssm_w_in = dense(ks[6], (na, D_MODEL, D_MODEL), D_MODEL)
    ssm_lambda_re = -0.5 + 0.01 * jax.random.normal(ks[7], (na, g, n), f32)
    ssm_lambda_im = (jnp.pi * jnp.arange(n, dtype=f32)
                     + 0.01 * jax.random.normal(ks[8], (na, g, n), f32))
    ssm_log_step = jax.random.uniform(ks[9], (na, g), f32,
                                      minval=math.log(STEP_MIN), maxval=math.log(STEP_MAX))
    ssm_b_re = dense(ks[10], (na, g, n, kk), 2 * kk)
    ssm_b_im = dense(ks[11], (na, g, n, kk), 2 * kk)
    ssm_c_re = dense(ks[12], (na, g, kk, n), 2 * n)
    ssm_c_im = dense(ks[13], (na, g, kk, n), 2 * n)
    ssm_d = jax.random.normal(ks[14], (na, D_MODEL), f32)
    ssm_w_glu = dense(ks[15], (na, D_MODEL, 2 * D_MODEL), D_MODEL)
    kv_in_norm = gain(ks[16], (D_MODEL,))
    mla_w_kv_a = dense(ks[17], (D_MODEL, KV_LORA_RANK + QK_ROPE_DIM), D_MODEL)
    mla_kv_a_norm = gain(ks[18], (KV_LORA_RANK,))
    mla_w_kv_b = dense(ks[19], (KV_LORA_RANK, N_HEADS * (QK_NOPE_DIM + V_DIM)), KV_LORA_RANK)
    mla_k_nope_norm = gain(ks[20], (QK_NOPE_DIM,))
    mla_k_rope_norm = gain(ks[21], (QK_ROPE_DIM,))
    mla_w_q_a = dense(ks[22], (nb, D_MODEL, Q_LORA_RANK), D_MODEL)
    mla_q_a_norm = gain(ks[23], (nb, Q_LORA_RANK))
    mla_w_q_b = dense(ks[24], (nb, Q_LORA_RANK, N_HEADS * QK_DIM), Q_LORA_RANK)
    mla_q_nope_norm = gain(ks[25], (nb, QK_NOPE_DIM))
    mla_q_rope_norm = gain(ks[26], (nb, QK_ROPE_DIM))
    mla_w_o = dense(ks[27], (nb, N_HEADS * V_DIM, D_MODEL), N_HEADS * V_DIM)
    return {
        'x': x, 'positions': positions,
        'mix_norm': mix_norm, 'ffn_norm': ffn_norm,
        'ffn_w_gate_up': ffn_w_gate_up, 'ffn_w_down': ffn_w_down,
        'ssm_w_in': ssm_w_in, 'ssm_lambda_re': ssm_lambda_re, 'ssm_lambda_im': ssm_lambda_im,
        'ssm_log_step': ssm_log_step, 'ssm_b_re': ssm_b_re, 'ssm_b_im': ssm_b_im,
        'ssm_c_re': ssm_c_re, 'ssm_c_im': ssm_c_im, 'ssm_d': ssm_d, 'ssm_w_glu': ssm_w_glu,
        'kv_in_norm': kv_in_norm, 'mla_w_kv_a': mla_w_kv_a, 'mla_kv_a_norm': mla_kv_a_norm,
        'mla_w_kv_b': mla_w_kv_b, 'mla_k_nope_norm': mla_k_nope_norm,
        'mla_k_rope_norm': mla_k_rope_norm,
        'mla_w_q_a': mla_w_q_a, 'mla_q_a_norm': mla_q_a_norm, 'mla_w_q_b': mla_w_q_b,
        'mla_q_nope_norm': mla_q_nope_norm, 'mla_q_rope_norm': mla_q_rope_norm,
        'mla_w_o': mla_w_o,
    }


def reference(x, positions, mix_norm, ffn_norm, ffn_w_gate_up, ffn_w_down,
              ssm_w_in, ssm_lambda_re, ssm_lambda_im, ssm_log_step, ssm_b_re, ssm_b_im,
              ssm_c_re, ssm_c_im, ssm_d, ssm_w_glu,
              kv_in_norm, mla_w_kv_a, mla_kv_a_norm, mla_w_kv_b, mla_k_nope_norm,
              mla_k_rope_norm, mla_w_q_a, mla_q_a_norm, mla_w_q_b, mla_q_nope_norm,
              mla_q_rope_norm, mla_w_o):
    bsz, seq, _ = x.shape
    cos, sin = rope_tables(positions)
    k_shared, v_shared = None, None
    for layer in range(DEPTH):
        h = rms_norm(x, mix_norm[layer])
        if layer < N_A_LAYERS:
            i = layer
            x = x + s5_mixer(h, ssm_w_in[i], ssm_lambda_re[i], ssm_lambda_im[i], ssm_log_step[i],
                             ssm_b_re[i], ssm_b_im[i], ssm_c_re[i], ssm_c_im[i], ssm_d[i],
                             ssm_w_glu[i])
        else:
            j = layer - N_A_LAYERS
            if j == 0:
                k_shared, v_shared = mla_shared_kv(rms_norm(x, kv_in_norm), mla_w_kv_a,
                                                   mla_kv_a_norm, mla_w_kv_b, mla_k_nope_norm,
                                                   mla_k_rope_norm, cos, sin)
            q = mla_queries(h, mla_w_q_a[j], mla_q_a_norm[j], mla_w_q_b[j],
                            mla_q_nope_norm[j], mla_q_rope_norm[j], cos, sin)
            o = causal_block_attention(q, k_shared, v_shared)
            o = o.transpose(0, 2, 1, 3).reshape(bsz, seq, N_HEADS * V_DIM)
            x = x + o @ mla_w_o[j]
        x = x + swiglu_ffn(rms_norm(x, ffn_norm[layer]), ffn_w_gate_up[layer], ffn_w_down[layer])
    return x
```

```python
import math
from contextlib import ExitStack

import numpy as np
import concourse.bass as bass
import concourse.mybir as mybir
from concourse.bass_utils import run_bass_kernel_spmd

F32 = mybir.dt.float32
BF16 = mybir.dt.bfloat16
I32 = mybir.dt.int32
U8 = mybir.dt.uint8
AF = mybir.ActivationFunctionType
ALU = mybir.AluOpType

D = 1024
SEQ = 2048
NCORE = 8
T = 512
DFF = 2816
NF = 22
EPS = 1e-6
TWO_PI = float(2 * np.pi)
ARENA_BYTES = 206 * 1024


class Buf:
    _n = 0
    __slots__ = ("ap", "id", "name", "excl")

    def __init__(self, ap, name="", excl=False):
        self.ap = ap
        Buf._n += 1
        self.id = Buf._n
        self.name = name
        self.excl = excl


class Op:
    __slots__ = ("eng", "fn", "deps", "dma", "lidx", "sem", "val", "gid", "slot_prev")


SEM_ROT = 30000


class Sched:
    ENGS = ("pe", "act", "dve", "pool", "sp")
    NSLOT = 8

    def __init__(self, nc, es):
        self.nc = nc
        self.es = es
        self.ops = []
        self.per = {e: [] for e in self.ENGS}
        self.last_w = {}
        self.readers = {}
        self.pending = {e: set() for e in self.ENGS}
        self.cnt = {e: 0 for e in self.ENGS}
        self.csems = {e: [] for e in self.ENGS}
        self.dcount = {e: 0 for e in self.ENGS}
        self.dsems = {e: [] for e in self.ENGS}
        self.dslot_cnt = {e: [0] * self.NSLOT for e in self.ENGS}
        self.dslot_last = {e: [None] * self.NSLOT for e in self.ENGS}
        self.nwait = 0

    def _newsem(self, name):
        return self.es.enter_context(self.nc.semaphore(name))

    def add(self, eng, fn, reads=(), writes=(), dma=False):
        op = Op()
        op.eng = eng
        op.fn = fn
        op.dma = dma
        deps = set(self.pending[eng])
        self.pending[eng] = set()
        ex = [b for b in reads if b.excl]
        if ex:
            reads = [b for b in reads if not b.excl]
            writes = list(writes) + ex
        for b in reads:
            w = self.last_w.get(b.id)
            if w is not None:
                deps.add(w)
        for b in writes:
            w = self.last_w.get(b.id)
            if w is not None:
                deps.add(w)
            deps.update(self.readers.get(b.id, ()))
        op.deps = deps
        op.gid = len(self.ops)
        op.lidx = len(self.per[eng])
        op.slot_prev = None
        if dma:
            k = self.dcount[eng] % self.NSLOT
            self.dcount[eng] += 1
            if not self.dsems[eng]:
                self.dsems[eng] = [self._newsem(f"d_{eng}_{i}") for i in range(self.NSLOT)]
            self.dslot_cnt[eng][k] += 1
            op.sem = self.dsems[eng][k]
            op.val = 16 * self.dslot_cnt[eng][k]
            op.slot_prev = self.dslot_last[eng][k]
            self.dslot_last[eng][k] = op.gid
        else:
            c = self.cnt[eng]
            si = c // SEM_ROT
            while len(self.csems[eng]) <= si:
                self.csems[eng].append(self._newsem(f"c_{eng}_{len(self.csems[eng])}"))
            op.sem = self.csems[eng][si]
            op.val = c % SEM_ROT + 1
            self.cnt[eng] = c + 1
        self.ops.append(op)
        self.per[eng].append(op)
        for b in reads:
            self.readers.setdefault(b.id, []).append(op.gid)
        for b in writes:
            self.last_w[b.id] = op.gid
            self.readers[b.id] = []
        return op

    def barrier(self):
        deps = set()
        for e in self.ENGS:
            for op in reversed(self.per[e]):
                if not op.dma:
                    deps.add(op.gid)
                    break
            for g in self.dslot_last[e]:
                if g is not None:
                    deps.add(g)
        for e in self.ENGS:
            self.pending[e] |= deps

    def finish(self):
        self.barrier()
        self.add("sp", None)

    def emit(self, block):
        hooks = {"pe": block.tensor, "act": block.scalar, "dve": block.vector,
                 "pool": block.gpsimd, "sp": block.sync}
        for e in self.ENGS:
            if self.per[e]:
                hooks[e](lambda h, e=e: self._emit_eng(e, h))

    def _emit_eng(self, e, h):
        waited = {}
        for op in self.per[e]:
            need = {}
            deps = set(op.deps)
            if op.slot_prev is not None:
                deps.add(op.slot_prev)
            for g in deps:
                d = self.ops[g]
                if d.fn is None:
                    continue
                if not d.dma and d.eng == e:
                    if e == "pe":
                        continue
                    if op.lidx - d.lidx > 2:
                        continue
                key = id(d.sem)
                if waited.get(key, 0) >= d.val:
                    continue
                if key not in need or need[key][1] < d.val:
                    need[key] = (d.sem, d.val)
            for key, (sem, val) in need.items():
                h.wait_ge(sem, val)
                waited[key] = val
                self.nwait += 1
            if op.fn is None:
                continue
            inst = op.fn(h)
            inst.then_inc(op.sem, 16 if op.dma else 1)


class Arena:
    def __init__(self, ap_u8, nbytes):
        self.ap = ap_u8
        self.n = nbytes
        self.off = 0
        self.peak = 0

    def reset(self, off=0):
        self.off = off

    def alloc(self, shape, dtype, name=""):
        esz = {F32: 4, BF16: 2, I32: 4, U8: 1}[dtype]
        free = int(np.prod(shape[1:]))
        nb = (free * esz + 63) // 64 * 64
        assert self.off + nb <= self.n, f"arena overflow {name}: {self.off}+{nb}>{self.n}"
        v = self.ap[:, self.off:self.off + free * esz]
        if dtype != U8:
            v = v.bitcast(dtype)
        if len(shape) > 2:
            names = " ".join(f"d{i}" for i in range(1, len(shape)))
            kw = {f"d{i}": shape[i] for i in range(1, len(shape) - 1)}
            v = v.rearrange(f"p ({names}) -> p {names}", **kw)
        if shape[0] < 128:
            v = v[0:shape[0]]
        self.off += nb
        self.peak = max(self.peak, self.off)
        return Buf(v, name)


class K:
    def __init__(self, nseq, stop_after=None, only=None):
        self.only = only
        self.nseq = nseq
        self.ntok = nseq * SEQ
        self.ntile = self.ntok // T
        self.stop_after = stop_after

    def dma(self, eng, out_ap, in_ap, reads=(), writes=()):
        self.S.add(eng, lambda h: h.dma_start(out=out_ap, in_=in_ap), reads, writes, dma=True)

    def mm(self, out_ap, lhsT, rhs, start, stop, reads, writes):
        self.S.add("pe", lambda h: h.matmul(out_ap, lhsT, rhs, start=start, stop=stop), reads, writes)

    def act(self, out_ap, in_ap, func, reads, writes, scale=1.0, bias=0.0):
        self.S.add("act", lambda h: h.activation(out=out_ap, in_=in_ap, func=func, bias=bias, scale=scale),
                   reads, writes)

    def tt(self, eng, out_ap, a, b, op, reads, writes):
        self.S.add(eng, lambda h: h.tensor_tensor(out=out_ap, in0=a, in1=b, op=op), reads, writes)

    def ts(self, eng, out_ap, a, s1, s2, op0, op1, reads, writes):
        if s2 is None:
            self.S.add(eng, lambda h: h.tensor_scalar(out=out_ap, in0=a, scalar1=s1, scalar2=None, op0=op0),
                       reads, writes)
        else:
            self.S.add(eng, lambda h: h.tensor_scalar(out=out_ap, in0=a, scalar1=s1, scalar2=s2, op0=op0, op1=op1),
                       reads, writes)

    def stt(self, eng, out_ap, a, sc, b, op0, op1, reads, writes):
        self.S.add(eng, lambda h: h.scalar_tensor_tensor(out=out_ap, in0=a, scalar=sc, in1=b, op0=op0, op1=op1),
                   reads, writes)

    def cp(self, eng, out_ap, in_ap, reads, writes):
        self.S.add(eng, lambda h: h.tensor_copy(out=out_ap, in_=in_ap), reads, writes)

    def recip(self, out_ap, in_ap, reads, writes):
        self.S.add("dve", lambda h: h.reciprocal(out=out_ap, in_=in_ap), reads, writes)

    def memset(self, eng, ap, val, writes):
        self.S.add(eng, lambda h: h.memset(ap, val), (), writes)

    def declare(self, nc):
        n = self.ntok

        def ein(name, shape, dt=F32):
            return nc.dram_tensor(name, list(shape), dt, kind="ExternalInput").ap()

        def scr(name, shape, dt):
            return nc.dram_tensor(name, list(shape), dt, kind="Internal").ap()

        d = {}
        d["xT"] = ein("xT", [D, n])
        d["pos"] = ein("pos", [n], I32)
        d["mixn"] = ein("mixn", [4, 128, 8])
        d["ffnn"] = ein("ffnn", [4, 128, 8])
        d["wgu"] = ein("wgu", [4, D, 2 * DFF])
        d["wdn"] = ein("wdn", [4, DFF, D])
        d["win"] = ein("win", [2, D, D])
        d["s5col"] = ein("s5col", [2, 128, 3, 32])
        d["s5row"] = ein("s5row", [2, 3, 4096])
        d["lbre"] = ein("lbre", [2, 128, 32, 128])
        d["lbim"] = ein("lbim", [2, 128, 32, 128])
        d["gre"] = ein("gre", [2, 128, 32, 128])
        d["gim"] = ein("gim", [2, 128, 32, 128])
        d["dsk"] = ein("dsk", [2, 128, 8])
        d["wglu"] = ein("wglu", [2, D, 2 * D])
        d["kvinn"] = ein("kvinn", [128, 8])
        d["wkva"] = ein("wkva", [D, 288])
        d["wkvasw"] = ein("wkvasw", [D, 32])
        d["kvan"] = ein("kvan", [128, 2])
        d["wkvbk"] = ein("wkvbk", [256, 1024])
        d["wkvbv"] = ein("wkvbv", [256, 1024])
        d["kn128"] = ein("kn128", [128, 1])
        d["kr32"] = ein("kr32", [32, 2])
        d["wqa"] = ein("wqa", [2, D, 384])
        d["qan"] = ein("qan", [2, 128, 3])
        d["wqb"] = ein("wqb", [2, 384, 1536])
        d["wqbsw"] = ein("wqbsw", [2, 384, 512])
        d["qg96"] = ein("qg96", [2, 96, 2])
        d["wo"] = ein("wo", [2, D, D])
        d["outT"] = nc.dram_tensor("outT", [D, n], F32, kind="ExternalOutput").ap()
        d["X"] = scr("X", [D, n], F32)
        d["U"] = scr("U", [D, n], BF16)
        d["Z"] = scr("Z", [D, n], BF16)
        d["TAB"] = scr("TAB", [8, 128, 4 * 2 * 512], F32)
        d["ROPE"] = scr("ROPE", [2, 32, n], F32)
        d["KT"] = scr("KT", [16, 96, n], BF16)
        d["V"] = scr("V", [n, 1024], BF16)
        d["QT"] = scr("QT", [16, 96, n], BF16)
        self.d = d

    def consts(self):
        A, S = self.A, self.S
        c = {}
        io_f = A.alloc([128, 512], F32, "io_f")
        io_p = A.alloc([128, 512], F32, "io_p")
        S.add("pool", lambda h: h.iota(io_f.ap, pattern=[[1, 512]], base=0, channel_multiplier=0,
                                       allow_small_or_imprecise_dtypes=True), (), [io_f])
        S.add("pool", lambda h: h.iota(io_p.ap, pattern=[[0, 512]], base=0, channel_multiplier=1,
                                       allow_small_or_imprecise_dtypes=True), (), [io_p])
        c["io_f"], c["io_p"] = io_f, io_p
        onesf = A.alloc([128, 128], BF16, "ones_full")
        self.memset("pool", onesf.ap, 1.0, [onesf])
        c["ones"] = onesf
        t_a = A.alloc([128, 128], F32, "t_a")
        t_b = A.alloc([128, 128], F32, "t_b")
        self.ts("dve", t_a.ap, io_p.ap[:, 0:128], 64.0, None, ALU.is_ge, None, [io_p], [t_a])
        self.ts("dve", t_b.ap, io_f.ap[:, 0:128], 64.0, None, ALU.is_ge, None, [io_f], [t_b])
        self.tt("dve", t_a.ap, t_a.ap, t_b.ap, ALU.is_equal, [t_a, t_b], [t_a])
        ob = A.alloc([128, 128], BF16, "ones64x2")
        self.cp("dve", ob.ap, t_a.ap, [t_a], [ob])
        c["ones64x2"] = ob
        qsc = A.alloc([128, 1], F32, "qscale")
        self.memset("pool", qsc.ap, 1.0 / 64, [qsc])
        self.memset("pool", qsc.ap[64:96], 1.0 / 32, [qsc])
        c["qscale"] = qsc
        mk = t_b
        self.tt("dve", mk.ap, io_p.ap[:, 0:128], io_f.ap[:, 0:128], ALU.is_le, [io_p, io_f], [mk])
        mkb = A.alloc([128, 128], BF16, "mask01")
        self.cp("dve", mkb.ap, mk.ap, [mk], [mkb])
        c["mask01"] = mkb
        self.c = c

    def range_reduce(self, eng, ang, tmpf, tmpi, cols=None):
        a, f, i = ang.ap, tmpf.ap, tmpi.ap
        self.ts(eng, f, a, 1.0 / TWO_PI, None, ALU.mult, None, [ang], [tmpf])
        self.cp(eng, i, f, [tmpf], [tmpi])
        self.cp(eng, f, i, [tmpi], [tmpf])
        self.stt(eng, a, f, -TWO_PI, a, ALU.mult, ALU.add, [tmpf, ang], [ang])
        self.ts(eng, a, a, float(np.pi), float(-np.pi), ALU.min, ALU.max, [ang], [ang])

    def rms_rstd(self, sq_parts, ones, nrows, ps, std, rstd, reads, scale):
        n = len(sq_parts)
        for i, (ap, kp) in enumerate(sq_parts):
            self.mm(ps.ap[0:nrows, :], ones.ap[0:kp, 0:nrows], ap, i == 0, i == n - 1, reads + [ones], [ps])
        self.act(std.ap[0:nrows, :], ps.ap[0:nrows, :], AF.Sqrt, [ps], [std], bias=self.epsb.ap[0:nrows, 0:1], scale=scale)
        self.recip(rstd.ap[0:nrows, :], std.ap[0:nrows, :], [std], [rstd])

    def rmsnorm_tile(self, xt, gain, h, sq, ps, std, rstd):
        self.act(sq.ap, xt.ap, AF.Square, [xt], [sq])
        self.rms_rstd([(sq.ap[:, c, :], 128) for c in range(8)], self.c["ones"], 128, ps, std, rstd, [sq], 1.0 / 1024)
        for c in range(8):
            self.stt("dve", h.ap[:, c, :], xt.ap[:, c, :], gain.ap[:, c:c + 1], rstd.ap, ALU.mult, ALU.mult,
                     [xt, gain, rstd], [h])

    def xview(self, X, t):
        return X.rearrange("(c p) t -> p c t", p=128)[:, :, t * T:(t + 1) * T]

    def load_w(self, buf, src, kc):
        v = src.rearrange("(k p) o -> p k o", p=128)
        for k in range(kc):
            self.dma("pool", buf.ap[:, k, :], v[:, k, :], (), [buf])

    def phase_begin(self):
        self.S.barrier()
        self.A.reset(self.arena_mark)

    def phase_ffn(self, layer, src, dst):
        A, S, d = self.A, self.S, self.d
        self.phase_begin()
        wgu = A.alloc([128, 8, 2 * DFF], BF16, "wgu")
        wdn = A.alloc([128, NF, D], BF16, "wdn")
        self.load_w(wgu, d["wgu"][layer], 8)
        self.load_w(wdn, d["wdn"][layer], NF)
        gain = A.alloc([128, 8], F32, "ffn_g")
        self.dma("sp", gain.ap, d["ffnn"][layer], (), [gain])
        xb = [A.alloc([128, 8, T], F32, f"x{i}") for i in range(2)]
        h = A.alloc([128, 8, T], BF16, "h")
        actb = A.alloc([128, NF, T], BF16, "act")
        sq = Buf(actb.ap[:, 0:8, :], "sq_alias")
        sq.id = actb.id
        std = A.alloc([128, T], F32, "std")
        rstd = A.alloc([128, T], F32, "rstd")
        sg = [A.alloc([128, T], F32, "sg0")] * 2
        PS = self.PS
        self.dma("sp", xb[0].ap, self.xview(src, 0), (), [xb[0]])
        for t in range(self.ntile):
            xt = xb[t % 2]
            if t + 1 < self.ntile:
                self.dma("sp", xb[(t + 1) % 2].ap, self.xview(src, t + 1), (), [xb[(t + 1) % 2]])
            self.rmsnorm_tile(xt, gain, h, sq, PS[6], std, rstd)
            for f in range(NF):
                g_ps, u_ps = PS[2 * (f % 2)], PS[2 * (f % 2) + 1]
                for k in range(8):
                    self.mm(g_ps.ap, wgu.ap[:, k, f * 128:(f + 1) * 128], h.ap[:, k, :], k == 0, k == 7, [wgu, h], [g_ps])
                for k in range(8):
                    self.mm(u_ps.ap, wgu.ap[:, k, DFF + f * 128:DFF + (f + 1) * 128], h.ap[:, k, :], k == 0, k == 7,
                            [wgu, h], [u_ps])
                s = sg[f % 2]
                self.act(s.ap, g_ps.ap, AF.Silu, [g_ps], [s])
                self.tt("dve", actb.ap[:, f, :], s.ap, u_ps.ap, ALU.mult, [s, u_ps], [actb])
            for m in range(8):
                o_ps = PS[4 + (m % 2)]
                for f in range(NF):
                    self.mm(o_ps.ap, wdn.ap[:, f, m * 128:(m + 1) * 128], actb.ap[:, f, :], f == 0, f == NF - 1,
                            [wdn, actb], [o_ps])
                self.tt("dve", xt.ap[:, m, :], xt.ap[:, m, :], o_ps.ap, ALU.add, [xt, o_ps], [xt])
            self.dma("sp", self.xview(dst, t), xt.ap, [xt], ())

    def phase_s5a(self, layer, src):
        A, d = self.A, self.d
        self.phase_begin()
        win = A.alloc([128, 8, D], BF16, "win")
        self.load_w(win, d["win"][layer], 8)
        gain = A.alloc([128, 8], F32, "mix_g")
        self.dma("sp", gain.ap, d["mixn"][layer], (), [gain])
        xb = [A.alloc([128, 8, T], F32, f"x{i}") for i in range(2)]
        h = A.alloc([128, 8, T], BF16, "h")
        sq = A.alloc([128, 8, T], BF16, "sq")
        ub = [A.alloc([128, 8, T], BF16, f"u{i}") for i in range(2)]
        std = A.alloc([128, T], F32, "std")
        rstd = A.alloc([128, T], F32, "rstd")
        PS = self.PS
        self.dma("sp", xb[0].ap, self.xview(src, 0), (), [xb[0]])
        for t in range(self.ntile):
            xt = xb[t % 2]
            if t + 1 < self.ntile:
                self.dma("sp", xb[(t + 1) % 2].ap, self.xview(src, t + 1), (), [xb[(t + 1) % 2]])
            self.rmsnorm_tile(xt, gain, h, sq, PS[6], std, rstd)
            u = ub[t % 2]
            for m in range(8):
                ps = PS[m % 4]
                for k in range(8):
                    self.mm(ps.ap, win.ap[:, k, m * 128:(m + 1) * 128], h.ap[:, k, :], k == 0, k == 7, [win, h], [ps])
                if m % 2 == 0:
                    self.act(u.ap[:, m, :], ps.ap, AF.Copy, [ps], [u])
                else:
                    self.cp("dve", u.ap[:, m, :], ps.ap, [ps], [u])
            self.dma("sp", self.xview(d["U"], t), u.ap, [u], ())

    def s5_coeffs(self, lr, li, ls, shape, want_f):
        A = self.A
        P, Fd = shape

        def new(nm, dt=F32):
            return A.alloc([P, Fd], dt, nm)
        step = new("step")
        self.act(step.ap, ls, AF.Exp, self._s5src, [step])
        th = new("th")
        self.tt("dve", th.ap, li, step.ap, ALU.mult, self._s5src + [step], [th])
        rho = new("rho")
        self.tt("dve", rho.ap, lr, step.ap, ALU.mult, self._s5src + [step], [rho])
        self.act(rho.ap, rho.ap, AF.Exp, [rho], [rho])
        tf, ti = new("tf"), new("ti", I32)
        self.range_reduce("dve", th, tf, ti, None)
        out = {"th": th, "rho": rho}
        if want_f:
            sn, cs = new("sn"), new("cs")
            self.act(sn.ap, th.ap, AF.Sin, [th], [sn])
            self.ts("dve", cs.ap, th.ap, float(np.pi / 2), None, ALU.add, None, [th], [cs])
            self.range_reduce("dve", cs, tf, ti, None)
            self.act(cs.ap, cs.ap, AF.Sin, [cs], [cs])
            self.tt("dve", cs.ap, cs.ap, rho.ap, ALU.mult, [cs, rho], [cs])
            self.ts("dve", cs.ap, cs.ap, -1.0, None, ALU.add, None, [cs], [cs])
            self.tt("dve", sn.ap, sn.ap, rho.ap, ALU.mult, [sn, rho], [sn])
            den = tf
            self.tt("dve", den.ap, lr, lr, ALU.mult, self._s5src, [den])
            t2 = step
            self.tt("dve", t2.ap, li, li, ALU.mult, self._s5src, [t2])
            self.tt("dve", den.ap, den.ap, t2.ap, ALU.add, [den, t2], [den])
            self.recip(den.ap, den.ap, [den], [den])
            fre, fim = new("fre"), new("fim")
            self.tt("dve", fre.ap, cs.ap, lr, ALU.mult, [cs] + self._s5src, [fre])
            self.tt("dve", t2.ap, sn.ap, li, ALU.mult, [sn] + self._s5src, [t2])
            self.tt("dve", fre.ap, fre.ap, t2.ap, ALU.add, [fre, t2], [fre])
            self.tt("dve", fre.ap, fre.ap, den.ap, ALU.mult, [fre, den], [fre])
            self.tt("dve", fim.ap, sn.ap, lr, ALU.mult, [sn] + self._s5src, [fim])
            self.tt("dve", t2.ap, cs.ap, li, ALU.mult, [cs] + self._s5src, [t2])
            self.tt("dve", fim.ap, fim.ap, t2.ap, ALU.subtract, [fim, t2], [fim])
            self.tt("dve", fim.ap, fim.ap, den.ap, ALU.mult, [fim, den], [fim])
            out["fre"], out["fim"] = fre, fim
        return out

    def phase_s5b(self, layer):
        A, S, d, c = self.A, self.S, self.d, self.c
        self.phase_begin()
        PS = self.PS
        bre = A.alloc([128, 32, 128], BF16, "bre")
        bim = A.alloc([128, 32, 128], BF16, "bim")
        c1 = A.alloc([128, 32, 128], BF16, "c1")
        c2 = A.alloc([128, 32, 128], BF16, "c2")
        c3 = A.alloc([128, 32, 128], BF16, "c3")
        rho = A.alloc([128, 32], F32, "rho_col")
        dcol = A.alloc([128, 8], F32, "dcol")
        self.dma("sp", dcol.ap, d["dsk"][layer], (), [dcol])
        mark = A.off
        QW = 1024
        rowp = A.alloc([128, 3, QW], F32, "rowp")
        lre = A.alloc([128, QW], F32, "lre")
        lim = A.alloc([128, QW], F32, "lim")
        markq = A.off
        for qt in range(4):
            A.reset(markq)
            for i3 in range(3):
                self.dma("sp", rowp.ap[:, i3, :], d["s5row"][layer, i3, qt * QW:(qt + 1) * QW].partition_broadcast(128),
                         (), [rowp])
            self._s5src = [rowp]
            co = self.s5_coeffs(rowp.ap[:, 0, :], rowp.ap[:, 1, :], rowp.ap[:, 2, :], (128, QW), True)
            fre, fim = co["fre"], co["fim"]
            prs = slice(qt * 8, qt * 8 + 8)
            self.dma("sp", lre.ap, d["lbre"][layer][:, prs, :].rearrange("p a b -> p (a b)"), (), [lre])
            self.dma("sp", lim.ap, d["lbim"][layer][:, prs, :].rearrange("p a b -> p (a b)"), (), [lim])
            t1, t2 = co["th"], co["rho"]
            bre_f = bre.ap[:, prs, :].rearrange("p a b -> p (a b)")
            bim_f = bim.ap[:, prs, :].rearrange("p a b -> p (a b)")
            self.tt("dve", t1.ap, lre.ap, fre.ap, ALU.mult, [lre, fre], [t1])
            self.tt("pool", t2.ap, lim.ap, fim.ap, ALU.mult, [lim, fim], [t2])
            self.tt("dve", bre_f, t1.ap, t2.ap, ALU.subtract, [t1, t2], [bre])
            self.tt("dve", t1.ap, lre.ap, fim.ap, ALU.mult, [lre, fim], [t1])
            self.tt("pool", t2.ap, lim.ap, fre.ap, ALU.mult, [lim, fre], [t2])
            self.tt("dve", bim_f, t1.ap, t2.ap, ALU.add, [t1, t2], [bim])
            self.dma("sp", lre.ap, d["gre"][layer][:, prs, :].rearrange("p a b -> p (a b)"), (), [lre])
            self.dma("sp", lim.ap, d["gim"][layer][:, prs, :].rearrange("p a b -> p (a b)"), (), [lim])
            self.cp("dve", c1.ap[:, prs, :].rearrange("p a b -> p (a b)"), lre.ap, [lre], [c1])
            self.act(c2.ap[:, prs, :].rearrange("p a b -> p (a b)"), lre.ap, AF.Copy, [lre], [c2], scale=-1.0)
            self.act(c3.ap[:, prs, :].rearrange("p a b -> p (a b)"), lim.ap, AF.Copy, [lim], [c3], scale=-1.0)
        S.barrier()
        A.reset(mark)
        colp = A.alloc([128, 3, 32], F32, "colp")
        self.dma("sp", colp.ap, d["s5col"][layer], (), [colp])
        self._s5src = [colp]
        co = self.s5_coeffs(colp.ap[:, 0, :], colp.ap[:, 1, :], colp.ap[:, 2, :], (128, 32), False)
        self.cp("dve", rho.ap, co["rho"].ap, [co["rho"]], [rho])
        th = co["th"]
        io1 = A.alloc([128, 512], F32, "io1")
        self.ts("dve", io1.ap, c["io_f"].ap, 1.0, None, ALU.add, None, [c["io_f"]], [io1])
        tabs = [A.alloc([128, 4, 2, 512], F32, f"tabw{i}") for i in range(2)]
        tf = A.alloc([128, 512], F32, "rr_f")
        ti = A.alloc([128, 512], I32, "rr_i")
        ang = A.alloc([128, 512], F32, "ang")
        for fc in range(8):
            tb = tabs[fc % 2]
            for pp in range(4):
                p = fc * 4 + pp
                self.ts("dve", ang.ap, io1.ap, th.ap[:, p:p + 1], None, ALU.mult, None, [io1, th], [ang])
                self.range_reduce("dve", ang, tf, ti, None)
                self.act(tb.ap[:, pp, 1, :], ang.ap, AF.Sin, [ang], [tb])
                self.ts("dve", ang.ap, io1.ap, th.ap[:, p:p + 1], float(np.pi / 2), ALU.mult, ALU.add, [io1, th], [ang])
                self.range_reduce("dve", ang, tf, ti, None)
                self.act(tb.ap[:, pp, 0, :], ang.ap, AF.Sin, [ang], [tb])
            self.dma("sp", d["TAB"][fc], tb.ap.rearrange("p a b c -> p (a b c)"), [tb], ())
        S.barrier()
        A.reset(mark)
        ubuf = [A.alloc([128, SEQ], BF16, f"ub{i}") for i in range(2)]
        tabb = [A.alloc([128, 4, 2, 512], F32, f"tab{i}") for i in range(2)]
        NW = 2
        wk = []
        for i in range(NW):
            wk.append(dict(
                t1=A.alloc([128, T], F32, f"w{i}t1"), t2=A.alloc([128, T], F32, f"w{i}t2"),
                t3=A.alloc([128, T], F32, f"w{i}t3"), t4=A.alloc([128, T], F32, f"w{i}t4"),
                zr=A.alloc([128, T], F32, f"w{i}zr"), zi=A.alloc([128, T], F32, f"w{i}zi"),
                q1=A.alloc([128, T], BF16, f"w{i}q1"), q2=A.alloc([128, T], BF16, f"w{i}q2"),
                q3=A.alloc([128, T], BF16, f"w{i}q3"), q4=A.alloc([128, T], BF16, f"w{i}q4")))
        ca = A.alloc([128, 1], F32, "ca")
        cb = A.alloc([128, 1], F32, "cb")
        yb = [A.alloc([128, T], F32, f"y{i}") for i in range(2)]
        sb = [A.alloc([128, T], F32, f"s{i}") for i in range(2)]
        zg = [A.alloc([128, T], BF16, f"zg{i}") for i in range(2)]
        xr = [[A.alloc([128, 1], F32, f"xr{p}_{i}") for i in range(2)] for p in range(4)]
        xi = [[A.alloc([128, 1], F32, f"xi{p}_{i}") for i in range(2)] for p in range(4)]
        groups = [(s, fc) for s in range(self.nseq) for fc in range(8)]
        units = [(gi, t, pp) for gi in range(len(groups)) for t in range(4) for pp in range(4)]
        gbuf = {}
        epc = [0]

        def load_group(gi):
            s_, fc_ = groups[gi]
            ub, tb = ubuf[gi % 2], tabb[gi % 2]
            self.dma("sp", ub.ap, d["U"][fc_ * 128:(fc_ + 1) * 128, s_ * SEQ:(s_ + 1) * SEQ], (), [ub])
            self.dma("sp", tb.ap.rearrange("p a b c -> p (a b c)"), d["TAB"][fc_], (), [tb])
            gbuf[gi] = (ub, tb)

        def ctx(i):
            gi, t, pp = units[i]
            if t == 0 and pp == 0 and gi not in gbuf:
                load_group(gi)
            if t == 1 and pp == 0 and gi + 1 < len(groups) and (gi + 1) not in gbuf:
                load_group(gi + 1)
            s_, fc_ = groups[gi]
            ub, tb = gbuf[gi]
            return dict(s=s_, fc=fc_, t=t, pp=pp, ub=ub, tb=tb, p=fc_ * 4 + pp, w=wk[i % NW],
                        cosT=tb.ap[:, pp, 0, :], sinT=tb.ap[:, pp, 1, :],
                        pre=PS[4 + 2 * (i % 2)], pim=PS[5 + 2 * (i % 2)])

        def scan(cx, zkey, tkey, carry):
            w, p, t, pp = cx["w"], cx["p"], cx["t"], cx["pp"]
            rb = rho.ap[:, p:p + 1].to_broadcast([128, T])
            if t == 0:
                S.add("dve", lambda h, o=w[zkey].ap, a=rb, b_=w[tkey].ap: h.tensor_tensor_scan(
                    out=o, data0=a, data1=b_, initial=0.0, op0=ALU.mult, op1=ALU.add), [rho, w[tkey]], [w[zkey]])
            else:
                cbuf = carry[pp][(t + 1) % 2]
                S.add("dve", lambda h, o=w[zkey].ap, a=rb, b_=w[tkey].ap, i0=cbuf.ap[:, 0:1]: h.tensor_tensor_scan(
                    out=o, data0=a, data1=b_, initial=i0, op0=ALU.mult, op1=ALU.add),
                    [rho, w[tkey], cbuf], [w[zkey]])

        def emit_pair(ia, ib):
            ca_ = ctx(ia) if ia is not None else None
            cb_ = ctx(ib) if ib is not None else None
            if ca_:
                w, tb, ub = ca_["w"], ca_["tb"], ca_["ub"]
                ucols = ub.ap[:, ca_["t"] * T:(ca_["t"] + 1) * T]
                self.mm(ca_["pre"].ap, bre.ap[:, ca_["p"], :], ucols, True, True, [bre, ub], [ca_["pre"]])
                self.mm(ca_["pim"].ap, bim.ap[:, ca_["p"], :], ucols, True, True, [bim, ub], [ca_["pim"]])
            def A(k):
                if not ca_:
                    return
                w, tb = ca_["w"], ca_["tb"]
                pre, pim, cosT, sinT = ca_["pre"], ca_["pim"], ca_["cosT"], ca_["sinT"]
                if k == 0:
                    self.tt("dve", w["t1"].ap, pre.ap, cosT, ALU.mult, [pre, tb], [w["t1"]])
                elif k == 1:
                    self.tt("dve", w["t4"].ap, pre.ap, sinT, ALU.mult, [pre, tb], [w["t4"]])
                elif k == 2:
                    self.tt("dve", w["t2"].ap, pim.ap, sinT, ALU.mult, [pim, tb], [w["t2"]])
                elif k == 3:
                    self.tt("dve", w["t3"].ap, pim.ap, cosT, ALU.mult, [pim, tb], [w["t3"]])
                elif k == 4:
                    self.tt("dve", w["t1"].ap, w["t1"].ap, w["t2"].ap, ALU.add, [w["t1"], w["t2"]], [w["t1"]])
                elif k == 5:
                    self.tt("dve", w["t3"].ap, w["t3"].ap, w["t4"].ap, ALU.subtract, [w["t3"], w["t4"]], [w["t3"]])
            def B(k):
                if not cb_:
                    return
                w, tb, cosT, sinT = cb_["w"], cb_["tb"], cb_["cosT"], cb_["sinT"]
                if k == 0:
                    scan(cb_, "zr", "t1", xr)
                elif k == 1:
                    scan(cb_, "zi", "t3", xi)
                elif k == 2:
                    self.tt("dve", w["q1"].ap, w["zr"].ap, cosT, ALU.mult, [w["zr"], tb], [w["q1"]])
                elif k == 3:
                    self.tt("dve", w["q2"].ap, w["zi"].ap, sinT, ALU.mult, [w["zi"], tb], [w["q2"]])
                elif k == 4:
                    self.tt("dve", w["q3"].ap, w["zr"].ap, sinT, ALU.mult, [w["zr"], tb], [w["q3"]])
                elif k == 5:
                    self.tt("dve", w["q4"].ap, w["zi"].ap, cosT, ALU.mult, [w["zi"], tb], [w["q4"]])
            for k in range(6):
                A(k)
                B(k)
            if not cb_:
                return
            w, tb, cosT, sinT, t, pp, p = cb_["w"], cb_["tb"], cb_["cosT"], cb_["sinT"], cb_["t"], cb_["pp"], cb_["p"]
            ub, fc_, s_ = cb_["ub"], cb_["fc"], cb_["s"]
            cur = t % 2
            if t < 3:
                cT, sT = cosT[:, T - 1:T], sinT[:, T - 1:T]
                zrl, zil = w["zr"].ap[:, T - 1:T], w["zi"].ap[:, T - 1:T]
                self.tt("pool", ca.ap, zrl, cT, ALU.mult, [w["zr"], tb], [ca])
                self.tt("pool", cb.ap, zil, sT, ALU.mult, [w["zi"], tb], [cb])
                self.tt("pool", xr[pp][cur].ap, ca.ap, cb.ap, ALU.subtract, [ca, cb], [xr[pp][cur]])
                self.tt("pool", ca.ap, zrl, sT, ALU.mult, [w["zr"], tb], [ca])
                self.tt("pool", cb.ap, zil, cT, ALU.mult, [w["zi"], tb], [cb])
                self.tt("pool", xi[pp][cur].ap, ca.ap, cb.ap, ALU.add, [ca, cb], [xi[pp][cur]])
            y = PS[t]
            self.mm(y.ap, c1.ap[:, p, :], w["q1"].ap, pp == 0, False, [c1, w["q1"]], [y])
            self.mm(y.ap, c2.ap[:, p, :], w["q2"].ap, False, False, [c2, w["q2"]], [y])
            self.mm(y.ap, c3.ap[:, p, :], w["q3"].ap, False, False, [c3, w["q3"]], [y])
            self.mm(y.ap, c3.ap[:, p, :], w["q4"].ap, False, pp == 3, [c3, w["q4"]], [y])
            if pp == 3:
                yy, ss, zz = yb[epc[0] % 2], sb[epc[0] % 2], zg[epc[0] % 2]
                epc[0] += 1
                self.stt("dve", yy.ap, ub.ap[:, t * T:(t + 1) * T], dcol.ap[:, fc_:fc_ + 1], PS[t].ap, ALU.mult, ALU.add,
                         [ub, dcol, PS[t]], [yy])
                self.act(ss.ap, yy.ap, AF.Square, [yy], [ss])
                self.act(ss.ap, ss.ap, AF.Identity, [ss], [ss], scale=0.044715, bias=self.oneb.ap[:, 0:1])
                self.tt("dve", ss.ap, ss.ap, yy.ap, ALU.mult, [ss, yy], [ss])
                self.act(ss.ap, ss.ap, AF.Sigmoid, [ss], [ss], scale=float(2.0 * math.sqrt(2.0 / math.pi)))
                self.tt("dve", zz.ap, yy.ap, ss.ap, ALU.mult, [yy, ss], [zz])
                tg = s_ * 4 + t
                self.dma("sp", d["Z"][fc_ * 128:(fc_ + 1) * 128, tg * T:(tg + 1) * T], zz.ap, [zz], ())

        nU = len(units)
        emit_pair(0, None)
        for i in range(nU):
            emit_pair(i + 1 if i + 1 < nU else None, i)

    def phase_s5c(self, layer, X):
        A, d = self.A, self.d
        self.phase_begin()
        PS = self.PS
        wg = A.alloc([128, 8, 2 * D], BF16, "wglu")
        self.load_w(wg, d["wglu"][layer], 8)
        xb = [A.alloc([128, 8, T], F32, f"x{i}") for i in range(2)]
        zb = [A.alloc([128, 8, T], BF16, f"z{i}") for i in range(2)]
        sg = [A.alloc([128, T], F32, f"sg{i}") for i in range(2)]
        self.dma("sp", xb[0].ap, self.xview(X, 0), (), [xb[0]])
        self.dma("sp", zb[0].ap, self.xview(d["Z"], 0), (), [zb[0]])
        for t in range(self.ntile):
            xt, zt = xb[t % 2], zb[t % 2]
            if t + 1 < self.ntile:
                self.dma("sp", xb[(t + 1) % 2].ap, self.xview(X, t + 1), (), [xb[(t + 1) % 2]])
                self.dma("sp", zb[(t + 1) % 2].ap, self.xview(d["Z"], t + 1), (), [zb[(t + 1) % 2]])
            for m in range(8):
                v_ps, g_ps = PS[2 * (m % 2)], PS[2 * (m % 2) + 1]
                for k in range(8):
                    self.mm(v_ps.ap, wg.ap[:, k, m * 128:(m + 1) * 128], zt.ap[:, k, :], k == 0, k == 7, [wg, zt], [v_ps])
                for k in range(8):
                    self.mm(g_ps.ap, wg.ap[:, k, D + m * 128:D + (m + 1) * 128], zt.ap[:, k, :], k == 0, k == 7,
                            [wg, zt], [g_ps])
                s = sg[m % 2]
                self.act(s.ap, g_ps.ap, AF.Sigmoid, [g_ps], [s])
                self.tt("dve", s.ap, s.ap, v_ps.ap, ALU.mult, [s, v_ps], [s])
                self.tt("pool", xt.ap[:, m, :], xt.ap[:, m, :], s.ap, ALU.add, [xt, s], [xt])
            self.dma("sp", self.xview(X, t), xt.ap, [xt], ())

    def phase_rope(self):
        A, S, d, c = self.A, self.S, self.d, self.c
        self.phase_begin()
        W = 2048
        pm = A.alloc([32, 1], F32, "pm")
        hi = A.alloc([32, 1], F32, "hi")
        self.ts("dve", hi.ap, c["io_p"].ap[0:32, 0:1], 16.0, 16.0, ALU.is_ge, ALU.mult, [c["io_p"]], [hi])
        self.tt("dve", pm.ap, c["io_p"].ap[0:32, 0:1], hi.ap, ALU.subtract, [c["io_p"], hi], [pm])
        invf = A.alloc([32, 1], F32, "invf")
        self.act(invf.ap, pm.ap, AF.Exp, [pm], [invf], scale=float(-math.log(10000.0) / 16.0))
        sgn = A.alloc([32, 1], F32, "sgn")
        self.ts("dve", sgn.ap, hi.ap, 1.0 / 8.0, -1.0, ALU.mult, ALU.add, [hi], [sgn])
        pi_ = A.alloc([32, W], I32, "pos_i")
        pf = A.alloc([32, W], F32, "pos_f")
        ang = A.alloc([32, W], F32, "ang")
        tf = A.alloc([32, W], F32, "tf")
        ti = A.alloc([32, W], I32, "ti")
        o = A.alloc([32, W], F32, "o")
        for j in range(self.ntok // W):
            self.dma("sp", pi_.ap, d["pos"][j * W:(j + 1) * W].partition_broadcast(32), (), [pi_])
            self.cp("dve", pf.ap, pi_.ap, [pi_], [pf])
            self.ts("dve", ang.ap, pf.ap, invf.ap[:, 0:1], None, ALU.mult, None, [pf, invf], [ang])
            self.range_reduce("dve", ang, tf, ti, None)
            self.act(o.ap, ang.ap, AF.Sin, [ang], [o])
            self.ts("dve", o.ap, o.ap, sgn.ap[:, 0:1], None, ALU.mult, None, [o, sgn], [o])
            self.dma("sp", d["ROPE"][1][:, j * W:(j + 1) * W], o.ap, [o], ())
            self.ts("dve", ang.ap, pf.ap, invf.ap[:, 0:1], float(np.pi / 2), ALU.mult, ALU.add, [pf, invf], [ang])
            self.range_reduce("dve", ang, tf, ti, None)
            self.act(o.ap, ang.ap, AF.Sin, [ang], [o])
            self.dma("sp", d["ROPE"][0][:, j * W:(j + 1) * W], o.ap, [o], ())

    def phase_kv(self, X):
        A, S, d, c = self.A, self.S, self.d, self.c
        self.phase_begin()
        PS = self.PS
        wa = A.alloc([128, 8, 288], BF16, "wkva")
        wasw = A.alloc([128, 8, 32], BF16, "wkvasw")
        wbk = A.alloc([128, 2, 1024], BF16, "wkvbk")
        wbv = A.alloc([128, 2, 1024], BF16, "wkvbv")
        self.load_w(wa, d["wkva"], 8)
        self.load_w(wasw, d["wkvasw"], 8)
        self.load_w(wbk, d["wkvbk"], 2)
        self.load_w(wbv, d["wkvbv"], 2)
        gin = A.alloc([128, 8], F32, "gin")
        ga = A.alloc([128, 2], F32, "ga")
        gk = A.alloc([128, 1], F32, "gk")
        gr = A.alloc([32, 2], F32, "gr")
        self.dma("sp", gin.ap, d["kvinn"], (), [gin])
        self.dma("sp", ga.ap, d["kvan"], (), [ga])
        self.dma("sp", gk.ap, d["kn128"], (), [gk])
        self.dma("sp", gr.ap, d["kr32"], (), [gr])
        xb = [A.alloc([128, 8, T], F32, f"x{i}") for i in range(2)]
        rp = [A.alloc([32, 2, T], F32, f"rp{i}") for i in range(2)]
        h = A.alloc([128, 8, T], BF16, "h")
        sq = A.alloc([128, 8, T], BF16, "sq")
        std = A.alloc([128, T], F32, "std")
        rstd = A.alloc([128, T], F32, "rstd")
        cf = A.alloc([128, 2, T], F32, "cf")
        cn = A.alloc([128, 2, T], BF16, "cn")
        kr = A.alloc([32, T], F32, "kr")
        krs = A.alloc([32, T], F32, "krs")
        kro = A.alloc([32, T], BF16, "kro")
        kn = [A.alloc([128, T], BF16, f"kn{i}") for i in range(2)]
        vb = [A.alloc([128, 1024], BF16, f"v{i}") for i in range(2)]
        rope_v = d["ROPE"].rearrange("a p t -> p a t")
        self.dma("sp", xb[0].ap, self.xview(X, 0), (), [xb[0]])
        self.dma("sp", rp[0].ap, rope_v[:, :, 0:T], (), [rp[0]])
        for t in range(self.ntile):
            xt, rt = xb[t % 2], rp[t % 2]
            if t + 1 < self.ntile:
                self.dma("sp", xb[(t + 1) % 2].ap, self.xview(X, t + 1), (), [xb[(t + 1) % 2]])
                self.dma("sp", rp[(t + 1) % 2].ap, rope_v[:, :, (t + 1) * T:(t + 2) * T], (), [rp[(t + 1) % 2]])
            self.rmsnorm_tile(xt, gin, h, sq, PS[6], std, rstd)
            for m in range(2):
                ps = PS[m]
                for k in range(8):
                    self.mm(ps.ap, wa.ap[:, k, m * 128:(m + 1) * 128], h.ap[:, k, :], k == 0, k == 7, [wa, h], [ps])
                self.cp("dve", cf.ap[:, m, :], ps.ap, [ps], [cf])
                self.act(sq.ap[:, m, :], ps.ap, AF.Square, [ps], [sq])
            self.rms_rstd([(sq.ap[:, m, :], 128) for m in range(2)], c["ones"], 128, PS[6], std, rstd, [sq], 1.0 / 256)
            for m in range(2):
                self.stt("dve", cn.ap[:, m, :], cf.ap[:, m, :], ga.ap[:, m:m + 1], rstd.ap, ALU.mult, ALU.mult,
                         [cf, ga, rstd], [cn])
            p1, p2 = PS[2], PS[3]
            for k in range(8):
                self.mm(p1.ap[0:32, :], wa.ap[:, k, 256:288], h.ap[:, k, :], k == 0, k == 7, [wa, h], [p1])
            for k in range(8):
                self.mm(p2.ap[0:32, :], wasw.ap[:, k, :], h.ap[:, k, :], k == 0, k == 7, [wasw, h], [p2])
            self.act(sq.ap[0:32, 2, :], p1.ap[0:32, :], AF.Square, [p1], [sq])
            self.rms_rstd([(sq.ap[0:32, 2, :], 32)], c["ones"], 32, PS[6], std, rstd, [sq], 1.0 / 32)
            self.stt("dve", kr.ap, p1.ap[0:32, :], gr.ap[:, 0:1], rstd.ap[0:32, :], ALU.mult, ALU.mult, [p1, gr, rstd], [kr])
            self.stt("dve", krs.ap, p2.ap[0:32, :], gr.ap[:, 1:2], rstd.ap[0:32, :], ALU.mult, ALU.mult, [p2, gr, rstd], [krs])
            self.tt("pool", kr.ap, kr.ap, rt.ap[:, 0, :], ALU.mult, [kr, rt], [kr])
            self.tt("pool", krs.ap, krs.ap, rt.ap[:, 1, :], ALU.mult, [krs, rt], [krs])
            self.tt("pool", kro.ap, kr.ap, krs.ap, ALU.add, [kr, krs], [kro])
            for hh in range(16):
                self.dma("sp", d["KT"][hh, 64:96, t * T:(t + 1) * T], kro.ap, [kro], ())
            for hp in range(8):
                ps = PS[hp % 2]
                for k in range(2):
                    self.mm(ps.ap, wbk.ap[:, k, hp * 128:(hp + 1) * 128], cn.ap[:, k, :], k == 0, k == 1, [wbk, cn], [ps])
                self.act(sq.ap[:, 3 + hp % 2, :], ps.ap, AF.Square, [ps], [sq])
                ms = PS[4 + hp % 2]
                self.mm(ms.ap, c["ones64x2"].ap, sq.ap[:, 3 + hp % 2, :], True, True, [c["ones64x2"], sq], [ms])
                self.act(std.ap, ms.ap, AF.Sqrt, [ms], [std], bias=self.epsb.ap[:, 0:1], scale=1.0 / 64)
                self.recip(rstd.ap, std.ap, [std], [rstd])
                kk = kn[hp % 2]
                self.stt("dve", kk.ap, ps.ap, gk.ap[:, 0:1], rstd.ap, ALU.mult, ALU.mult, [ps, gk, rstd], [kk])
                self.dma("sp", d["KT"][2 * hp, 0:64, t * T:(t + 1) * T], kk.ap[0:64, :], [kk], ())
                self.dma("sp", d["KT"][2 * hp + 1, 0:64, t * T:(t + 1) * T], kk.ap[64:128, :], [kk], ())
            for tb_ in range(4):
                vv = vb[tb_ % 2]
                for half in range(2):
                    ps = PS[2 + half]
                    for k in range(2):
                        self.mm(ps.ap, cn.ap[:, k, tb_ * 128:(tb_ + 1) * 128], wbv.ap[:, k, half * 512:(half + 1) * 512],
                                k == 0, k == 1, [cn, wbv], [ps])
                    if half == 0:
                        self.act(vv.ap[:, 0:512], ps.ap, AF.Copy, [ps], [vv])
                    else:
                        self.cp("dve", vv.ap[:, 512:1024], ps.ap, [ps], [vv])
                r0 = t * T + tb_ * 128
                self.dma("sp", d["V"][r0:r0 + 128, :], vv.ap, [vv], ())

    def phase_q(self, j, X):
        A, S, d, c = self.A, self.S, self.d, self.c
        self.phase_begin()
        PS = self.PS
        layer = 2 + j
        wqa = A.alloc([128, 8, 384], BF16, "wqa")
        wqb = A.alloc([128, 3, 1536], BF16, "wqb")
        wqs = A.alloc([128, 3, 512], BF16, "wqbsw")
        self.load_w(wqa, d["wqa"][j], 8)
        self.load_w(wqb, d["wqb"][j], 3)
        self.load_w(wqs, d["wqbsw"][j], 3)
        gm = A.alloc([128, 8], F32, "gm")
        gq = A.alloc([128, 3], F32, "gq")
        g96 = A.alloc([96, 2], F32, "g96")
        self.dma("sp", gm.ap, d["mixn"][layer], (), [gm])
        self.dma("sp", gq.ap, d["qan"][j], (), [gq])
        self.dma("sp", g96.ap, d["qg96"][j], (), [g96])
        xb = [A.alloc([128, 8, T], F32, f"x{i}") for i in range(2)]
        rp = [A.alloc([96, 2, T], F32, f"rp{i}") for i in range(2)]
        h = A.alloc([128, 8, T], BF16, "h")
        sq = A.alloc([128, 8, T], BF16, "sq")
        std = A.alloc([128, T], F32, "std")
        rstd = A.alloc([128, T], F32, "rstd")
        cqf = A.alloc([128, 3, T], F32, "cqf")
        cqn = A.alloc([128, 3, T], BF16, "cqn")
        qT = [A.alloc([96, 16, T], BF16, f"qT{i}") for i in range(2)]
        qr = [A.alloc([96, T], F32, f"qr{i}") for i in range(2)]
        qs = [A.alloc([96, T], F32, f"qs{i}") for i in range(2)]
        sqq = [A.alloc([96, T], BF16, f"sqq{i}") for i in range(2)]
        stq = [A.alloc([96, T], F32, f"stq{i}") for i in range(2)]
        rsq = [A.alloc([96, T], F32, f"rsq{i}") for i in range(2)]
        rope_v = d["ROPE"].rearrange("a p t -> p a t")
        qt_v = d["QT"].rearrange("h p t -> p h t")
        self.dma("sp", xb[0].ap, self.xview(X, 0), (), [xb[0]])
        self.dma("sp", rp[0].ap[64:96], rope_v[:, :, 0:T], (), [rp[0]])
        for t in range(self.ntile):
            xt, rt, qq = xb[t % 2], rp[t % 2], qT[t % 2]
            if t + 1 < self.ntile:
                self.dma("sp", xb[(t + 1) % 2].ap, self.xview(X, t + 1), (), [xb[(t + 1) % 2]])
                self.dma("sp", rp[(t + 1) % 2].ap[64:96], rope_v[:, :, (t + 1) * T:(t + 2) * T], (), [rp[(t + 1) % 2]])
            self.rmsnorm_tile(xt, gm, h, sq, PS[6], std, rstd)
            for m in range(3):
                ps = PS[m]
                for k in range(8):
                    self.mm(ps.ap, wqa.ap[:, k, m * 128:(m + 1) * 128], h.ap[:, k, :], k == 0, k == 7, [wqa, h], [ps])
                self.cp("dve", cqf.ap[:, m, :], ps.ap, [ps], [cqf])
                self.act(sq.ap[:, m, :], ps.ap, AF.Square, [ps], [sq])
            self.rms_rstd([(sq.ap[:, m, :], 128) for m in range(3)], c["ones"], 128, PS[6], std, rstd, [sq], 1.0 / 384)
            for m in range(3):
                self.stt("dve", cqn.ap[:, m, :], cqf.ap[:, m, :], gq.ap[:, m:m + 1], rstd.ap, ALU.mult, ALU.mult,
                         [cqf, gq, rstd], [cqn])
            def q_front(hh):
                e = hh % 2
                qp, qsw, ms = PS[e * 3], PS[e * 3 + 1], PS[e * 3 + 2]
                for k in range(3):
                    self.mm(qp.ap[0:96, :], wqb.ap[:, k, hh * 96:(hh + 1) * 96], cqn.ap[:, k, :], k == 0, k == 2, [wqb, cqn], [qp])
                for k in range(3):
                    self.mm(qsw.ap[64:96, :], wqs.ap[:, k, hh * 32:(hh + 1) * 32], cqn.ap[:, k, :], k == 0, k == 2,
                            [wqs, cqn], [qsw])
                sq_, st_, rs_ = sqq[e], stq[e], rsq[e]
                self.act(sq_.ap, qp.ap[0:96, :], AF.Square, [qp], [sq_])
                self.mm(ms.ap[0:96, :], c["ones64x2"].ap[0:96, 0:96], sq_.ap, True, True, [c["ones64x2"], sq_], [ms])
                self.act(st_.ap, ms.ap[0:96, :], AF.Sqrt, [ms], [st_], bias=self.epsb.ap[0:96, 0:1],
                         scale=c["qscale"].ap[0:96, 0:1])
                self.recip(rs_.ap, st_.ap, [st_], [rs_])

            def q_back(hh):
                e = hh % 2
                qp, qsw = PS[e * 3], PS[e * 3 + 1]
                rs_, qr_, qs_ = rsq[e], qr[e], qs[e]
                self.stt("dve", qq.ap[0:64, hh, :], qp.ap[0:64, :], g96.ap[0:64, 0:1], rs_.ap[0:64, :], ALU.mult, ALU.mult,
                         [qp, g96, rs_], [qq])
                self.stt("dve", qr_.ap[64:96, :], qp.ap[64:96, :], g96.ap[64:96, 0:1], rs_.ap[64:96, :], ALU.mult, ALU.mult,
                         [qp, g96, rs_], [qr_])
                self.stt("dve", qs_.ap[64:96, :], qsw.ap[64:96, :], g96.ap[64:96, 1:2], rs_.ap[64:96, :], ALU.mult, ALU.mult,
                         [qsw, g96, rs_], [qs_])
                self.tt("pool", qr_.ap[64:96, :], qr_.ap[64:96, :], rt.ap[64:96, 0, :], ALU.mult, [qr_, rt], [qr_])
                self.tt("pool", qs_.ap[64:96, :], qs_.ap[64:96, :], rt.ap[64:96, 1, :], ALU.mult, [qs_, rt], [qs_])
                self.tt("pool", qq.ap[64:96, hh, :], qr_.ap[64:96, :], qs_.ap[64:96, :], ALU.add, [qr_, qs_], [qq])

            for hh in range(16):
                q_front(hh)
                q_back(hh)
            self.dma("sp", qt_v[:, :, t * T:(t + 1) * T], qq.ap, [qq], ())

    def phase_att(self, j, X):
        A, S, d, c = self.A, self.S, self.d, self.c
        self.phase_begin()
        PS = self.PS
        wo = A.alloc([128, 8, D], BF16, "wo")
        self.load_w(wo, d["wo"][j], 8)
        ktb = A.alloc([96, 16, SEQ], BF16, "kt")
        vbuf = A.alloc([128, 16, 1024], BF16, "v")
        onesb = A.alloc([128, 64], BF16, "ones64")
        self.memset("pool", onesb.ap, 1.0, [onesb])
        xt = A.alloc([128, 8, T], F32, "x")
        qT = [A.alloc([96, 16, T], BF16, f"qT{i}") for i in range(2)]
        NP = 4
        pT = [A.alloc([128, T], BF16, f"pT{i}") for i in range(NP)]
        oT = A.alloc([128, 8, T], BF16, "oT")
        rden = [A.alloc([128, T], F32, f"rden{i}") for i in range(2)]
        qt_v = d["QT"].rearrange("h p t -> p h t")
        scale = 1.0 / math.sqrt(96.0)
        pcount = 0
        self.dma("sp", qT[0].ap, qt_v[:, :, 0:T], (), [qT[0]])
        for t in range(self.ntile):
            s, tq = t // 4, t % 4
            if tq == 0:
                self.dma("sp", ktb.ap, d["KT"].rearrange("h p t -> p h t")[:, :, s * SEQ:(s + 1) * SEQ], (), [ktb])
                self.dma("sp", vbuf.ap, d["V"][s * SEQ:(s + 1) * SEQ, :].rearrange("(k p) f -> p k f", p=128), (), [vbuf])
            qq = qT[t % 2]
            if t + 1 < self.ntile:
                self.dma("sp", qT[(t + 1) % 2].ap, qt_v[:, :, (t + 1) * T:(t + 2) * T], (), [qT[(t + 1) % 2]])
            self.dma("sp", xt.ap, self.xview(X, t), (), [xt])
            nkt = 4 * tq + 4
            units = [(hp, hl, kt) for hp in range(8) for hl in range(2) for kt in range(nkt)]
            LA = 2
            info = {}
            for i in range(len(units) + LA):
                if i < len(units):
                    hp, hl, kt = units[i]
                    hh = 2 * hp + hl
                    i_ = kt - 4 * tq
                    c0 = max(i_, 0) * 128
                    s_ps = PS[pcount % 4]
                    pt = pT[pcount % NP]
                    pcount += 1
                    self.mm(s_ps.ap[:, c0:T], ktb.ap[:, hh, kt * 128:(kt + 1) * 128], qq.ap[:, hh, c0:T], True, True,
                            [ktb, qq], [s_ps])
                    self.act(pt.ap[:, c0:T], s_ps.ap[:, c0:T], AF.Exp, [s_ps], [pt], scale=scale)
                    if i_ >= 0:
                        self.tt("pool", pt.ap[:, c0:c0 + 128], pt.ap[:, c0:c0 + 128], c["mask01"].ap, ALU.mult,
                                [pt, c["mask01"]], [pt])
                    info[i] = (pt, c0)
                j_ = i - LA
                if j_ >= 0:
                    hp, hl, kt = units[j_]
                    hh = 2 * hp + hl
                    pt, c0 = info.pop(j_)
                    rows = slice(64 * hl, 64 * hl + 64)
                    o_ps, d_ps = PS[4 + 2 * (hp % 2)], PS[5 + 2 * (hp % 2)]
                    self.mm(o_ps.ap[rows, c0:T], vbuf.ap[:, kt, hh * 64:(hh + 1) * 64], pt.ap[:, c0:T],
                            kt == 0, kt == nkt - 1, [vbuf, pt], [o_ps])
                    self.mm(d_ps.ap[rows, c0:T], onesb.ap, pt.ap[:, c0:T], kt == 0, kt == nkt - 1, [onesb, pt], [d_ps])
                    if hl == 1 and kt == nkt - 1:
                        rd = rden[hp % 2]
                        self.recip(rd.ap, d_ps.ap, [d_ps], [rd])
                        self.tt("dve", oT.ap[:, hp, :], o_ps.ap, rd.ap, ALU.mult, [o_ps, rd], [oT])
            for m in range(8):
                ps = PS[m % 4]
                for k in range(8):
                    self.mm(ps.ap, wo.ap[:, k, m * 128:(m + 1) * 128], oT.ap[:, k, :], k == 0, k == 7, [wo, oT], [ps])
                self.tt("dve", xt.ap[:, m, :], xt.ap[:, m, :], ps.ap, ALU.add, [xt, ps], [xt])
            self.dma("sp", self.xview(X, t), xt.ap, [xt], ())

    def build(self):
        nc = bass.Bass("TRN2", target_bir_lowering=False)
        self.nc = nc
        self.declare(nc)
        d = self.d
        with ExitStack() as es:
            arena_t = es.enter_context(nc.sbuf_tensor("arena", [128, ARENA_BYTES], U8))
            ps_t = es.enter_context(nc.psum_tensor("ps", [128, 8, 512], F32))
            self.S = Sched(nc, es)
            self.A = Arena(arena_t[:, :], ARENA_BYTES)
            self.PS = [Buf(ps_t[:, b, :], f"ps{b}", excl=True) for b in range(8)]
            self.epsb = self.A.alloc([128, 1], F32, "eps")
            self.memset("pool", self.epsb.ap, EPS, [self.epsb])
            self.oneb = self.A.alloc([128, 1], F32, "one")
            self.memset("pool", self.oneb.ap, 1.0, [self.oneb])
            self.consts()
            self.arena_mark = self.A.off
            X = d["X"]
            stages = []
            for l in range(2):
                stages += [("s5a", l), ("s5b", l), ("s5c", l), ("ffn", l)]
            stages += [("rope", 0), ("kv", 0)]
            for j in range(2):
                stages += [("q", j), ("att", j), ("ffn", 2 + j)]
            if self.stop_after is not None:
                stages = stages[:self.stop_after]
            if self.only is not None:
                stages = [stages[i] for i in self.only]
            first = True
            for si, (kind, idx) in enumerate(stages):
                last = si == len(stages) - 1
                if kind == "s5a":
                    self.phase_s5a(idx, d["xT"] if first else X)
                elif kind == "s5b":
                    self.phase_s5b(idx)
                elif kind == "s5c":
                    self.phase_s5c_src = d["xT"] if idx == 0 else X
                    self.phase_s5c2(idx, self.phase_s5c_src, d["outT"] if last else X)
                elif kind == "ffn":
                    self.phase_ffn(idx, X, d["outT"] if last else X)
                elif kind == "rope":
                    self.phase_rope()
                elif kind == "kv":
                    self.phase_kv(X)
                elif kind == "q":
                    self.phase_q(idx, X)
                elif kind == "att":
                    self.phase_att(idx, X)
                first = False
            if stages[-1][0] not in ("ffn", "s5c"):
                self.S.barrier()
                for tt_ in range(self.ntile):
                    self.dma("sp", self.xview(d["outT"], tt_), self.xview(X, tt_), (), ())
            self.S.finish()
            block = es.enter_context(nc.Block())
            self.S.emit(block)
        return nc

    def phase_s5c2(self, layer, src, dst):
        A, d = self.A, self.d
        self.phase_begin()
        PS = self.PS
        wg = A.alloc([128, 8, 2 * D], BF16, "wglu")
        self.load_w(wg, d["wglu"][layer], 8)
        xb = [A.alloc([128, 8, T], F32, f"x{i}") for i in range(2)]
        zb = [A.alloc([128, 8, T], BF16, f"z{i}") for i in range(2)]
        sg = [A.alloc([128, T], F32, f"sg{i}") for i in range(2)]
        self.dma("sp", xb[0].ap, self.xview(src, 0), (), [xb[0]])
        self.dma("sp", zb[0].ap, self.xview(d["Z"], 0), (), [zb[0]])
        for t in range(self.ntile):
            xt, zt = xb[t % 2], zb[t % 2]
            if t + 1 < self.ntile:
                self.dma("sp", xb[(t + 1) % 2].ap, self.xview(src, t + 1), (), [xb[(t + 1) % 2]])
                self.dma("sp", zb[(t + 1) % 2].ap, self.xview(d["Z"], t + 1), (), [zb[(t + 1) % 2]])
            for m in range(8):
                v_ps, g_ps = PS[2 * (m % 2)], PS[2 * (m % 2) + 1]
                for k in range(8):
                    self.mm(v_ps.ap, wg.ap[:, k, m * 128:(m + 1) * 128], zt.ap[:, k, :], k == 0, k == 7, [wg, zt], [v_ps])
                for k in range(8):
                    self.mm(g_ps.ap, wg.ap[:, k, D + m * 128:D + (m + 1) * 128], zt.ap[:, k, :], k == 0, k == 7,
                            [wg, zt], [g_ps])
                s = sg[m % 2]
                self.act(s.ap, g_ps.ap, AF.Sigmoid, [g_ps], [s])
                self.tt("dve", s.ap, s.ap, v_ps.ap, ALU.mult, [s, v_ps], [s])
                self.tt("pool", xt.ap[:, m, :], xt.ap[:, m, :], s.ap, ALU.add, [xt, s], [xt])
            self.dma("sp", self.xview(dst, t), xt.ap, [xt], ())


def _pc(v, nchunk):
    v = np.asarray(v, np.float32)
    return np.ascontiguousarray(v.reshape(v.shape[:-1] + (nchunk, 128)).swapaxes(-1, -2))


def host_layout(inp, nseq, core):
    f = lambda a: np.ascontiguousarray(np.asarray(a, np.float32))
    x = np.asarray(inp["x"], np.float32)[core * nseq:(core + 1) * nseq]
    m = {}
    m["xT"] = np.ascontiguousarray(x.reshape(nseq * SEQ, D).T)
    m["pos"] = np.ascontiguousarray(np.asarray(inp["positions"], np.int32)[core * nseq:(core + 1) * nseq].reshape(-1))
    return m


def host_shared(inp):
    f = lambda a: np.ascontiguousarray(np.asarray(a, np.float32))
    m = {}
    m["mixn"] = _pc(inp["mix_norm"], 8)
    m["ffnn"] = _pc(inp["ffn_norm"], 8)
    m["wgu"] = f(inp["ffn_w_gate_up"])
    m["wdn"] = f(inp["ffn_w_down"])
    m["win"] = f(inp["ssm_w_in"])
    lr, li, ls = f(inp["ssm_lambda_re"]), f(inp["ssm_lambda_im"]), f(inp["ssm_log_step"])
    lsb = np.broadcast_to(ls[:, :, None], lr.shape)
    def colrow(a):
        a = a.reshape(2, 32, 128)
        return a
    cols = np.stack([colrow(lr), colrow(li), colrow(lsb)], axis=1)
    m["s5col"] = np.ascontiguousarray(cols.transpose(0, 3, 1, 2))
    m["s5row"] = np.ascontiguousarray(cols.reshape(2, 3, 4096))
    bre, bim = f(inp["ssm_b_re"]), f(inp["ssm_b_im"])
    cre, cim = f(inp["ssm_c_re"]), f(inp["ssm_c_im"])
    def pad_b(b):
        out = np.zeros((2, 128, 32, 128), np.float32)
        for g in range(64):
            p, gp = g // 2, g % 2
            gl = g % 8
            out[:, gl * 16:(gl + 1) * 16, p, gp * 64:(gp + 1) * 64] = b[:, g].transpose(0, 2, 1)
        return out
    def pad_c(cc):
        out = np.zeros((2, 128, 32, 128), np.float32)
        for g in range(64):
            p, gp = g // 2, g % 2
            gl = g % 8
            out[:, gp * 64:(gp + 1) * 64, p, gl * 16:(gl + 1) * 16] = cc[:, g].transpose(0, 2, 1)
        return out
    m["lbre"], m["lbim"] = pad_b(bre), pad_b(bim)
    m["gre"], m["gim"] = pad_c(cre), pad_c(cim)
    m["dsk"] = _pc(inp["ssm_d"], 8)
    m["wglu"] = f(inp["ssm_w_glu"])
    m["kvinn"] = _pc(inp["kv_in_norm"], 8)
    wkva = f(inp["mla_w_kv_a"])
    m["wkva"] = wkva
    m["wkvasw"] = np.ascontiguousarray(np.concatenate([wkva[:, 272:288], wkva[:, 256:272]], axis=1))
    m["kvan"] = _pc(inp["mla_kv_a_norm"], 2)
    wkvb = f(inp["mla_w_kv_b"]).reshape(256, 16, 128)
    m["wkvbk"] = np.ascontiguousarray(wkvb[:, :, 0:64].reshape(256, 1024))
    m["wkvbv"] = np.ascontiguousarray(wkvb[:, :, 64:128].reshape(256, 1024))
    kn = f(inp["mla_k_nope_norm"])
    m["kn128"] = np.ascontiguousarray(np.concatenate([kn, kn]).reshape(128, 1))
    krn = f(inp["mla_k_rope_norm"])
    m["kr32"] = np.ascontiguousarray(np.stack([krn, np.concatenate([krn[16:], krn[:16]])], axis=1))
    m["wqa"] = f(inp["mla_w_q_a"])
    m["qan"] = _pc(inp["mla_q_a_norm"], 3)
    wqb = f(inp["mla_w_q_b"])
    m["wqb"] = wqb
    w4 = wqb.reshape(2, 384, 16, 96)
    m["wqbsw"] = np.ascontiguousarray(np.concatenate([w4[..., 80:96], w4[..., 64:80]], axis=-1).reshape(2, 384, 512))
    qn, qrn = f(inp["mla_q_nope_norm"]), f(inp["mla_q_rope_norm"])
    g0 = np.concatenate([qn, qrn], axis=1)
    g1 = np.concatenate([qn, qrn[:, 16:], qrn[:, :16]], axis=1)
    m["qg96"] = np.ascontiguousarray(np.stack([g0, g1], axis=2))
    m["wo"] = f(inp["mla_w_o"])
    return m


_CACHE = {}


def run(inputs, nseq_per_core, ncore, stop_after=None):
    key = (nseq_per_core, stop_after)
    if key not in _CACHE:
        _CACHE[key] = K(nseq_per_core, stop_after).build()
    nc = _CACHE[key]
    shared = host_shared(inputs)
    in_maps = []
    for c in range(ncore):
        mm_ = dict(shared)
        mm_.update(host_layout(inputs, nseq_per_core, c))
        in_maps.append(mm_)
    res = run_bass_kernel_spmd(nc, in_maps, core_ids=list(range(ncore)))
    outs = [np.asarray(r["outT"]).T.reshape(nseq_per_core, SEQ, D) for r in res.results]
    return np.ascontiguousarray(np.concatenate(outs, axis=0).astype(np.float32))


def kernel(**inputs):
    return run(inputs, 4, NCORE)
```

```python
import math
from contextlib import ExitStack

import numpy as np
import concourse.bass as bass
import concourse.mybir as mybir
from concourse.bass_utils import run_bass_kernel_spmd

F32 = mybir.dt.float32
BF16 = mybir.dt.bfloat16
I32 = mybir.dt.int32
U8 = mybir.dt.uint8
AF = mybir.ActivationFunctionType
ALU = mybir.AluOpType

D = 1024
SEQ = 2048
NCORE = 8
T = 512
DFF = 2816
NF = 22
EPS = 1e-6
TWO_PI = float(2 * np.pi)
ARENA_BYTES = 206 * 1024


class Buf:
    _n = 0
    __slots__ = ("ap", "id", "name", "excl")

    def __init__(self, ap, name="", excl=False):
        self.ap = ap
        Buf._n += 1
        self.id = Buf._n
        self.name = name
        self.excl = excl


class Op:
    __slots__ = ("eng", "fn", "deps", "dma", "lidx", "sem", "val", "gid", "slot_prev")


SEM_ROT = 30000


class Sched:
    ENGS = ("pe", "act", "dve", "pool", "sp")
    NSLOT = 8

    def __init__(self, nc, es):
        self.nc = nc
        self.es = es
        self.ops = []
        self.per = {e: [] for e in self.ENGS}
        self.last_w = {}
        self.readers = {}
        self.pending = {e: set() for e in self.ENGS}
        self.cnt = {e: 0 for e in self.ENGS}
        self.csems = {e: [] for e in self.ENGS}
        self.dcount = {e: 0 for e in self.ENGS}
        self.dsems = {e: [] for e in self.ENGS}
        self.dslot_cnt = {e: [0] * self.NSLOT for e in self.ENGS}
        self.dslot_last = {e: [None] * self.NSLOT for e in self.ENGS}
        self.nwait = 0

    def _newsem(self, name):
        return self.es.enter_context(self.nc.semaphore(name))

    def add(self, eng, fn, reads=(), writes=(), dma=False):
        op = Op()
        op.eng = eng
        op.fn = fn
        op.dma = dma
        deps = set(self.pending[eng])
        self.pending[eng] = set()
        ex = [b for b in reads if b.excl]
        if ex:
            reads = [b for b in reads if not b.excl]
            writes = list(writes) + ex
        for b in reads:
            w = self.last_w.get(b.id)
            if w is not None:
                deps.add(w)
        for b in writes:
            w = self.last_w.get(b.id)
            if w is not None:
                deps.add(w)
            deps.update(self.readers.get(b.id, ()))
        op.deps = deps
        op.gid = len(self.ops)
        op.lidx = len(self.per[eng])
        op.slot_prev = None
        if dma:
            k = self.dcount[eng] % self.NSLOT
            self.dcount[eng] += 1
            if not self.dsems[eng]:
                self.dsems[eng] = [self._newsem(f"d_{eng}_{i}") for i in range(self.NSLOT)]
            self.dslot_cnt[eng][k] += 1
            op.sem = self.dsems[eng][k]
            op.val = 16 * self.dslot_cnt[eng][k]
            op.slot_prev = self.dslot_last[eng][k]
            self.dslot_last[eng][k] = op.gid
        else:
            c = self.cnt[eng]
            si = c // SEM_ROT
            while len(self.csems[eng]) <= si:
                self.csems[eng].append(self._newsem(f"c_{eng}_{len(self.csems[eng])}"))
            op.sem = self.csems[eng][si]
            op.val = c % SEM_ROT + 1
            self.cnt[eng] = c + 1
        self.ops.append(op)
        self.per[eng].append(op)
        for b in reads:
            self.readers.setdefault(b.id, []).append(op.gid)
        for b in writes:
            self.last_w[b.id] = op.gid
            self.readers[b.id] = []
        return op

    def barrier(self):
        deps = set()
        for e in self.ENGS:
            for op in reversed(self.per[e]):
                if not op.dma:
                    deps.add(op.gid)
                    break
            for g in self.dslot_last[e]:
                if g is not None:
                    deps.add(g)
        for e in self.ENGS:
            self.pending[e] |= deps

    def finish(self):
        self.barrier()
        self.add("sp", None)

    def emit(self, block):
        hooks = {"pe": block.tensor, "act": block.scalar, "dve": block.vector,
                 "pool": block.gpsimd, "sp": block.sync}
        for e in self.ENGS:
            if self.per[e]:
                hooks[e](lambda h, e=e: self._emit_eng(e, h))

    def _emit_eng(self, e, h):
        waited = {}
        for op in self.per[e]:
            need = {}
            deps = set(op.deps)
            if op.slot_prev is not None:
                deps.add(op.slot_prev)
            for g in deps:
                d = self.ops[g]
                if d.fn is None:
                    continue
                if not d.dma and d.eng == e:
                    if e == "pe":
                        continue
                    if op.lidx - d.lidx > 2:
                        continue
                key = id(d.sem)
                if waited.get(key, 0) >= d.val:
                    continue
                if key not in need or need[key][1] < d.val:
                    need[key] = (d.sem, d.val)
            for key, (sem, val) in need.items():
                h.wait_ge(sem, val)
                waited[key] = val
                self.nwait += 1
            if op.fn is None:
                continue
            inst = op.fn(h)
            inst.then_inc(op.sem, 16 if op.dma else 1)


class Arena:
    def __init__(self, ap_u8, nbytes):
        self.ap = ap_u8
        self.n = nbytes
        self.off = 0
        self.peak = 0

    def reset(self, off=0):
        self.off = off

    def alloc(self, shape, dtype, name=""):
        esz = {F32: 4, BF16: 2, I32: 4, U8: 1}[dtype]
        free = int(np.prod(shape[1:]))
        nb = (free * esz + 63) // 64 * 64
        assert self.off + nb <= self.n, f"arena overflow {name}: {self.off}+{nb}>{self.n}"
        v = self.ap[:, self.off:self.off + free * esz]
        if dtype != U8:
            v = v.bitcast(dtype)
        if len(shape) > 2:
            names = " ".join(f"d{i}" for i in range(1, len(shape)))
            kw = {f"d{i}": shape[i] for i in range(1, len(shape) - 1)}
            v = v.rearrange(f"p ({names}) -> p {names}", **kw)
        if shape[0] < 128:
            v = v[0:shape[0]]
        self.off += nb
        self.peak = max(self.peak, self.off)
        return Buf(v, name)


class K:
    def __init__(self, nseq, stop_after=None, only=None):
        self.only = only
        self.nseq = nseq
        self.ntok = nseq * SEQ
        self.ntile = self.ntok // T
        self.stop_after = stop_after

    def dma(self, eng, out_ap, in_ap, reads=(), writes=()):
        self.S.add(eng, lambda h: h.dma_start(out=out_ap, in_=in_ap), reads, writes, dma=True)

    def mm(self, out_ap, lhsT, rhs, start, stop, reads, writes):
        self.S.add("pe", lambda h: h.matmul(out_ap, lhsT, rhs, start=start, stop=stop), reads, writes)

    def act(self, out_ap, in_ap, func, reads, writes, scale=1.0, bias=0.0):
        self.S.add("act", lambda h: h.activation(out=out_ap, in_=in_ap, func=func, bias=bias, scale=scale),
                   reads, writes)

    def tt(self, eng, out_ap, a, b, op, reads, writes):
        self.S.add(eng, lambda h: h.tensor_tensor(out=out_ap, in0=a, in1=b, op=op), reads, writes)

    def ts(self, eng, out_ap, a, s1, s2, op0, op1, reads, writes):
        if s2 is None:
            self.S.add(eng, lambda h: h.tensor_scalar(out=out_ap, in0=a, scalar1=s1, scalar2=None, op0=op0),
                       reads, writes)
        else:
            self.S.add(eng, lambda h: h.tensor_scalar(out=out_ap, in0=a, scalar1=s1, scalar2=s2, op0=op0, op1=op1),
                       reads, writes)

    def stt(self, eng, out_ap, a, sc, b, op0, op1, reads, writes):
        self.S.add(eng, lambda h: h.scalar_tensor_tensor(out=out_ap, in0=a, scalar=sc, in1=b, op0=op0, op1=op1),
                   reads, writes)

    def cp(self, eng, out_ap, in_ap, reads, writes):
        self.S.add(eng, lambda h: h.tensor_copy(out=out_ap, in_=in_ap), reads, writes)

    def recip(self, out_ap, in_ap, reads, writes):
        self.S.add("dve", lambda h: h.reciprocal(out=out_ap, in_=in_ap), reads, writes)

    def memset(self, eng, ap, val, writes):
        self.S.add(eng, lambda h: h.memset(ap, val), (), writes)

    def declare(self, nc):
        n = self.ntok

        def ein(name, shape, dt=F32):
            return nc.dram_tensor(name, list(shape), dt, kind="ExternalInput").ap()

        def scr(name, shape, dt):
            return nc.dram_tensor(name, list(shape), dt, kind="Internal").ap()

        d = {}
        d["xT"] = ein("xT", [D, n])
        d["pos"] = ein("pos", [n], I32)
        d["mixn"] = ein("mixn", [4, 128, 8])
        d["ffnn"] = ein("ffnn", [4, 128, 8])
        d["wgu"] = ein("wgu", [4, D, 2 * DFF])
        d["wdn"] = ein("wdn", [4, DFF, D])
        d["win"] = ein("win", [2, D, D])
        d["s5col"] = ein("s5col", [2, 128, 3, 32])
        d["s5row"] = ein("s5row", [2, 3, 4096])
        d["lbre"] = ein("lbre", [2, 128, 32, 128])
        d["lbim"] = ein("lbim", [2, 128, 32, 128])
        d["gre"] = ein("gre", [2, 128, 32, 128])
        d["gim"] = ein("gim", [2, 128, 32, 128])
        d["dsk"] = ein("dsk", [2, 128, 8])
        d["wglu"] = ein("wglu", [2, D, 2 * D])
        d["kvinn"] = ein("kvinn", [128, 8])
        d["wkva"] = ein("wkva", [D, 288])
        d["wkvasw"] = ein("wkvasw", [D, 32])
        d["kvan"] = ein("kvan", [128, 2])
        d["wkvbk"] = ein("wkvbk", [256, 1024])
        d["wkvbv"] = ein("wkvbv", [256, 1024])
        d["kn128"] = ein("kn128", [128, 1])
        d["kr32"] = ein("kr32", [32, 2])
        d["wqa"] = ein("wqa", [2, D, 384])
        d["qan"] = ein("qan", [2, 128, 3])
        d["wqb"] = ein("wqb", [2, 384, 1536])
        d["wqbsw"] = ein("wqbsw", [2, 384, 512])
        d["qg96"] = ein("qg96", [2, 96, 2])
        d["wo"] = ein("wo", [2, D, D])
        d["outT"] = nc.dram_tensor("outT", [D, n], F32, kind="ExternalOutput").ap()
        d["X"] = scr("X", [D, n], F32)
        d["U"] = scr("U", [D, n], BF16)
        d["Z"] = scr("Z", [D, n], BF16)
        d["TAB"] = scr("TAB", [8, 128, 4 * 2 * 512], F32)
        d["ROPE"] = scr("ROPE", [2, 32, n], F32)
        d["KT"] = scr("KT", [16, 96, n], BF16)
        d["V"] = scr("V", [n, 1024], BF16)
        d["QT"] = scr("QT", [16, 96, n], BF16)
        self.d = d

    def consts(self):
        A, S = self.A, self.S
        c = {}
        io_f = A.alloc([128, 512], F32, "io_f")
        io_p = A.alloc([128, 512], F32, "io_p")
        S.add("pool", lambda h: h.iota(io_f.ap, pattern=[[1, 512]], base=0, channel_multiplier=0,
                                       allow_small_or_imprecise_dtypes=True), (), [io_f])
        S.add("pool", lambda h: h.iota(io_p.ap, pattern=[[0, 512]], base=0, channel_multiplier=1,
                                       allow_small_or_imprecise_dtypes=True), (), [io_p])
        c["io_f"], c["io_p"] = io_f, io_p
        onesf = A.alloc([128, 128], BF16, "ones_full")
        self.memset("pool", onesf.ap, 1.0, [onesf])
        c["ones"] = onesf
        t_a = A.alloc([128, 128], F32, "t_a")
        t_b = A.alloc([128, 128], F32, "t_b")
        self.ts("dve", t_a.ap, io_p.ap[:, 0:128], 64.0, None, ALU.is_ge, None, [io_p], [t_a])
        self.ts("dve", t_b.ap, io_f.ap[:, 0:128], 64.0, None, ALU.is_ge, None, [io_f], [t_b])
        self.tt("dve", t_a.ap, t_a.ap, t_b.ap, ALU.is_equal, [t_a, t_b], [t_a])
        ob = A.alloc([128, 128], BF16, "ones64x2")
        self.cp("dve", ob.ap, t_a.ap, [t_a], [ob])
        c["ones64x2"] = ob
        qsc = A.alloc([128, 1], F32, "qscale")
        self.memset("pool", qsc.ap, 1.0 / 64, [qsc])
        self.memset("pool", qsc.ap[64:96], 1.0 / 32, [qsc])
        c["qscale"] = qsc
        mk = t_b
        self.tt("dve", mk.ap, io_p.ap[:, 0:128], io_f.ap[:, 0:128], ALU.is_le, [io_p, io_f], [mk])
        mkb = A.alloc([128, 128], BF16, "mask01")
        self.cp("dve", mkb.ap, mk.ap, [mk], [mkb])
        c["mask01"] = mkb
        self.c = c

    def range_reduce(self, eng, ang, tmpf, tmpi, cols=None):
        a, f, i = ang.ap, tmpf.ap, tmpi.ap
        self.ts(eng, f, a, 1.0 / TWO_PI, None, ALU.mult, None, [ang], [tmpf])
        self.cp(eng, i, f, [tmpf], [tmpi])
        self.cp(eng, f, i, [tmpi], [tmpf])
        self.stt(eng, a, f, -TWO_PI, a, ALU.mult, ALU.add, [tmpf, ang], [ang])
        self.ts(eng, a, a, float(np.pi), float(-np.pi), ALU.min, ALU.max, [ang], [ang])

    def rms_rstd(self, sq_parts, ones, nrows, ps, std, rstd, reads, scale):
        n = len(sq_parts)
        for i, (ap, kp) in enumerate(sq_parts):
            self.mm(ps.ap[0:nrows, :], ones.ap[0:kp, 0:nrows], ap, i == 0, i == n - 1, reads + [ones], [ps])
        self.act(std.ap[0:nrows, :], ps.ap[0:nrows, :], AF.Ln, [ps], [std], bias=self.epsb.ap[0:nrows, 0:1], scale=scale)
        self.act(rstd.ap[0:nrows, :], std.ap[0:nrows, :], AF.Exp, [std], [rstd], scale=-0.5)

    def rmsnorm_tile(self, xt, gain, h, sq, ps, std, rstd):
        self.act(sq.ap, xt.ap, AF.Square, [xt], [sq])
        self.rms_rstd([(sq.ap[:, c, :], 128) for c in range(8)], self.c["ones"], 128, ps, std, rstd, [sq], 1.0 / 1024)
        for c in range(8):
            self.stt("dve", h.ap[:, c, :], xt.ap[:, c, :], gain.ap[:, c:c + 1], rstd.ap, ALU.mult, ALU.mult,
                     [xt, gain, rstd], [h])

    def xview(self, X, t):
        return X.rearrange("(c p) t -> p c t", p=128)[:, :, t * T:(t + 1) * T]

    def load_w(self, buf, src, kc):
        v = src.rearrange("(k p) o -> p k o", p=128)
        for k in range(kc):
            self.dma("pool", buf.ap[:, k, :], v[:, k, :], (), [buf])

    def phase_begin(self):
        self.S.barrier()
        self.A.reset(self.arena_mark)

    def phase_ffn(self, layer, src, dst):
        A, S, d = self.A, self.S, self.d
        self.phase_begin()
        wgu = A.alloc([128, 8, 2 * DFF], BF16, "wgu")
        wdn = A.alloc([128, NF, D], BF16, "wdn")
        self.load_w(wgu, d["wgu"][layer], 8)
        self.load_w(wdn, d["wdn"][layer], NF)
        gain = A.alloc([128, 8], F32, "ffn_g")
        self.dma("sp", gain.ap, d["ffnn"][layer], (), [gain])
        xb = [A.alloc([128, 8, T], F32, f"x{i}") for i in range(2)]
        h = A.alloc([128, 8, T], BF16, "h")
        actb = A.alloc([128, NF, T], BF16, "act")
        sq = Buf(actb.ap[:, 0:8, :], "sq_alias")
        sq.id = actb.id
        std = A.alloc([128, T], F32, "std")
        rstd = A.alloc([128, T], F32, "rstd")
        sg = [A.alloc([128, T], F32, "sg0")] * 2
        PS = self.PS
        self.dma("sp", xb[0].ap, self.xview(src, 0), (), [xb[0]])
        for t in range(self.ntile):
            xt = xb[t % 2]
            if t + 1 < self.ntile:
                self.dma("sp", xb[(t + 1) % 2].ap, self.xview(src, t + 1), (), [xb[(t + 1) % 2]])
            self.rmsnorm_tile(xt, gain, h, sq, PS[6], std, rstd)
            for f in range(NF):
                g_ps, u_ps = PS[2 * (f % 2)], PS[2 * (f % 2) + 1]
                for k in range(8):
                    self.mm(g_ps.ap, wgu.ap[:, k, f * 128:(f + 1) * 128], h.ap[:, k, :], k == 0, k == 7, [wgu, h], [g_ps])
                for k in range(8):
                    self.mm(u_ps.ap, wgu.ap[:, k, DFF + f * 128:DFF + (f + 1) * 128], h.ap[:, k, :], k == 0, k == 7,
                            [wgu, h], [u_ps])
                s = sg[f % 2]
                self.act(s.ap, g_ps.ap, AF.Silu, [g_ps], [s])
                self.tt("dve", actb.ap[:, f, :], s.ap, u_ps.ap, ALU.mult, [s, u_ps], [actb])
            for m in range(8):
                o_ps = PS[4 + (m % 2)]
                for f in range(NF):
                    self.mm(o_ps.ap, wdn.ap[:, f, m * 128:(m + 1) * 128], actb.ap[:, f, :], f == 0, f == NF - 1,
                            [wdn, actb], [o_ps])
                self.tt("dve", xt.ap[:, m, :], xt.ap[:, m, :], o_ps.ap, ALU.add, [xt, o_ps], [xt])
            self.dma("sp", self.xview(dst, t), xt.ap, [xt], ())

    def phase_s5a(self, layer, src):
        A, d = self.A, self.d
        self.phase_begin()
        win = A.alloc([128, 8, D], BF16, "win")
        self.load_w(win, d["win"][layer], 8)
        gain = A.alloc([128, 8], F32, "mix_g")
        self.dma("sp", gain.ap, d["mixn"][layer], (), [gain])
        xb = [A.alloc([128, 8, T], F32, f"x{i}") for i in range(2)]
        h = A.alloc([128, 8, T], BF16, "h")
        sq = A.alloc([128, 8, T], BF16, "sq")
        ub = [A.alloc([128, 8, T], BF16, f"u{i}") for i in range(2)]
        std = A.alloc([128, T], F32, "std")
        rstd = A.alloc([128, T], F32, "rstd")
        PS = self.PS
        self.dma("sp", xb[0].ap, self.xview(src, 0), (), [xb[0]])
        for t in range(self.ntile):
            xt = xb[t % 2]
            if t + 1 < self.ntile:
                self.dma("sp", xb[(t + 1) % 2].ap, self.xview(src, t + 1), (), [xb[(t + 1) % 2]])
            self.rmsnorm_tile(xt, gain, h, sq, PS[6], std, rstd)
            u = ub[t % 2]
            for m in range(8):
                ps = PS[m % 4]
                for k in range(8):
                    self.mm(ps.ap, win.ap[:, k, m * 128:(m + 1) * 128], h.ap[:, k, :], k == 0, k == 7, [win, h], [ps])
                if m % 2 == 0:
                    self.act(u.ap[:, m, :], ps.ap, AF.Copy, [ps], [u])
                else:
                    self.cp("dve", u.ap[:, m, :], ps.ap, [ps], [u])
            self.dma("sp", self.xview(d["U"], t), u.ap, [u], ())

    def s5_coeffs(self, lr, li, ls, shape, want_f):
        A = self.A
        P, Fd = shape

        def new(nm, dt=F32):
            return A.alloc([P, Fd], dt, nm)
        step = new("step")
        self.act(step.ap, ls, AF.Exp, self._s5src, [step])
        th = new("th")
        self.tt("dve", th.ap, li, step.ap, ALU.mult, self._s5src + [step], [th])
        rho = new("rho")
        self.tt("dve", rho.ap, lr, step.ap, ALU.mult, self._s5src + [step], [rho])
        self.act(rho.ap, rho.ap, AF.Exp, [rho], [rho])
        tf, ti = new("tf"), new("ti", I32)
        self.range_reduce("dve", th, tf, ti, None)
        out = {"th": th, "rho": rho}
        if want_f:
            sn, cs = new("sn"), new("cs")
            self.act(sn.ap, th.ap, AF.Sin, [th], [sn])
            self.ts("dve", cs.ap, th.ap, float(np.pi / 2), None, ALU.add, None, [th], [cs])
            self.range_reduce("dve", cs, tf, ti, None)
            self.act(cs.ap, cs.ap, AF.Sin, [cs], [cs])
            self.tt("dve", cs.ap, cs.ap, rho.ap, ALU.mult, [cs, rho], [cs])
            self.ts("dve", cs.ap, cs.ap, -1.0, None, ALU.add, None, [cs], [cs])
            self.tt("dve", sn.ap, sn.ap, rho.ap, ALU.mult, [sn, rho], [sn])
            den = tf
            self.tt("dve", den.ap, lr, lr, ALU.mult, self._s5src, [den])
            t2 = step
            self.tt("dve", t2.ap, li, li, ALU.mult, self._s5src, [t2])
            self.tt("dve", den.ap, den.ap, t2.ap, ALU.add, [den, t2], [den])
            self.recip(den.ap, den.ap, [den], [den])
            fre, fim = new("fre"), new("fim")
            self.tt("dve", fre.ap, cs.ap, lr, ALU.mult, [cs] + self._s5src, [fre])
            self.tt("dve", t2.ap, sn.ap, li, ALU.mult, [sn] + self._s5src, [t2])
            self.tt("dve", fre.ap, fre.ap, t2.ap, ALU.add, [fre, t2], [fre])
            self.tt("dve", fre.ap, fre.ap, den.ap, ALU.mult, [fre, den], [fre])
            self.tt("dve", fim.ap, sn.ap, lr, ALU.mult, [sn] + self._s5src, [fim])
            self.tt("dve", t2.ap, cs.ap, li, ALU.mult, [cs] + self._s5src, [t2])
            self.tt("dve", fim.ap, fim.ap, t2.ap, ALU.subtract, [fim, t2], [fim])
            self.tt("dve", fim.ap, fim.ap, den.ap, ALU.mult, [fim, den], [fim])
            out["fre"], out["fim"] = fre, fim
        return out

    def phase_s5b(self, layer):
        A, S, d, c = self.A, self.S, self.d, self.c
        self.phase_begin()
        PS = self.PS
        bre = A.alloc([128, 32, 128], BF16, "bre")
        bim = A.alloc([128, 32, 128], BF16, "bim")
        c1 = A.alloc([128, 32, 128], BF16, "c1")
        c2 = A.alloc([128, 32, 128], BF16, "c2")
        c3 = A.alloc([128, 32, 128], BF16, "c3")
        rho = A.alloc([128, 32], F32, "rho_col")
        dcol = A.alloc([128, 8], F32, "dcol")
        self.dma("sp", dcol.ap, d["dsk"][layer], (), [dcol])
        mark = A.off
        QW = 1024
        rowp = A.alloc([128, 3, QW], F32, "rowp")
        lre = A.alloc([128, QW], F32, "lre")
        lim = A.alloc([128, QW], F32, "lim")
        markq = A.off
        for qt in range(4):
            A.reset(markq)
            for i3 in range(3):
                self.dma("sp", rowp.ap[:, i3, :], d["s5row"][layer, i3, qt * QW:(qt + 1) * QW].partition_broadcast(128),
                         (), [rowp])
            self._s5src = [rowp]
            co = self.s5_coeffs(rowp.ap[:, 0, :], rowp.ap[:, 1, :], rowp.ap[:, 2, :], (128, QW), True)
            fre, fim = co["fre"], co["fim"]
            prs = slice(qt * 8, qt * 8 + 8)
            self.dma("sp", lre.ap, d["lbre"][layer][:, prs, :].rearrange("p a b -> p (a b)"), (), [lre])
            self.dma("sp", lim.ap, d["lbim"][layer][:, prs, :].rearrange("p a b -> p (a b)"), (), [lim])
            t1, t2 = co["th"], co["rho"]
            bre_f = bre.ap[:, prs, :].rearrange("p a b -> p (a b)")
            bim_f = bim.ap[:, prs, :].rearrange("p a b -> p (a b)")
            self.tt("dve", t1.ap, lre.ap, fre.ap, ALU.mult, [lre, fre], [t1])
            self.tt("pool", t2.ap, lim.ap, fim.ap, ALU.mult, [lim, fim], [t2])
            self.tt("dve", bre_f, t1.ap, t2.ap, ALU.subtract, [t1, t2], [bre])
            self.tt("dve", t1.ap, lre.ap, fim.ap, ALU.mult, [lre, fim], [t1])
            self.tt("pool", t2.ap, lim.ap, fre.ap, ALU.mult, [lim, fre], [t2])
            self.tt("dve", bim_f, t1.ap, t2.ap, ALU.add, [t1, t2], [bim])
            self.dma("sp", lre.ap, d["gre"][layer][:, prs, :].rearrange("p a b -> p (a b)"), (), [lre])
            self.dma("sp", lim.ap, d["gim"][layer][:, prs, :].rearrange("p a b -> p (a b)"), (), [lim])
            self.cp("dve", c1.ap[:, prs, :].rearrange("p a b -> p (a b)"), lre.ap, [lre], [c1])
            self.act(c2.ap[:, prs, :].rearrange("p a b -> p (a b)"), lre.ap, AF.Copy, [lre], [c2], scale=-1.0)
            self.act(c3.ap[:, prs, :].rearrange("p a b -> p (a b)"), lim.ap, AF.Copy, [lim], [c3], scale=-1.0)
        S.barrier()
        A.reset(mark)
        colp = A.alloc([128, 3, 32], F32, "colp")
        self.dma("sp", colp.ap, d["s5col"][layer], (), [colp])
        self._s5src = [colp]
        co = self.s5_coeffs(colp.ap[:, 0, :], colp.ap[:, 1, :], colp.ap[:, 2, :], (128, 32), False)
        self.cp("dve", rho.ap, co["rho"].ap, [co["rho"]], [rho])
        th = co["th"]
        io1 = A.alloc([128, 512], F32, "io1")
        self.ts("dve", io1.ap, c["io_f"].ap, 1.0, None, ALU.add, None, [c["io_f"]], [io1])
        tabs = [A.alloc([128, 4, 2, 512], F32, f"tabw{i}") for i in range(2)]
        tf = A.alloc([128, 512], F32, "rr_f")
        ti = A.alloc([128, 512], I32, "rr_i")
        ang = A.alloc([128, 512], F32, "ang")
        for fc in range(8):
            tb = tabs[fc % 2]
            for pp in range(4):
                p = fc * 4 + pp
                self.ts("dve", ang.ap, io1.ap, th.ap[:, p:p + 1], None, ALU.mult, None, [io1, th], [ang])
                self.range_reduce("dve", ang, tf, ti, None)
                self.act(tb.ap[:, pp, 1, :], ang.ap, AF.Sin, [ang], [tb])
                self.ts("dve", ang.ap, io1.ap, th.ap[:, p:p + 1], float(np.pi / 2), ALU.mult, ALU.add, [io1, th], [ang])
                self.range_reduce("dve", ang, tf, ti, None)
                self.act(tb.ap[:, pp, 0, :], ang.ap, AF.Sin, [ang], [tb])
            self.dma("sp", d["TAB"][fc], tb.ap.rearrange("p a b c -> p (a b c)"), [tb], ())
        S.barrier()
        A.reset(mark)
        ubuf = [A.alloc([128, SEQ], BF16, f"ub{i}") for i in range(2)]
        tabb = [A.alloc([128, 4, 2, 512], F32, f"tab{i}") for i in range(2)]
        NW = 2
        wk = []
        for i in range(NW):
            wk.append(dict(
                t1=A.alloc([128, T], F32, f"w{i}t1"), t2=A.alloc([128, T], F32, f"w{i}t2"),
                t3=A.alloc([128, T], F32, f"w{i}t3"), t4=A.alloc([128, T], F32, f"w{i}t4"),
                zr=A.alloc([128, T], F32, f"w{i}zr"), zi=A.alloc([128, T], F32, f"w{i}zi"),
                q1=A.alloc([128, T], BF16, f"w{i}q1"), q2=A.alloc([128, T], BF16, f"w{i}q2"),
                q3=A.alloc([128, T], BF16, f"w{i}q3"), q4=A.alloc([128, T], BF16, f"w{i}q4")))
        ca = A.alloc([128, 1], F32, "ca")
        cb = A.alloc([128, 1], F32, "cb")
        yb = [A.alloc([128, T], F32, f"y{i}") for i in range(2)]
        sb = [A.alloc([128, T], F32, f"s{i}") for i in range(2)]
        zg = [A.alloc([128, T], BF16, f"zg{i}") for i in range(2)]
        xr = [[A.alloc([128, 1], F32, f"xr{p}_{i}") for i in range(2)] for p in range(4)]
        xi = [[A.alloc([128, 1], F32, f"xi{p}_{i}") for i in range(2)] for p in range(4)]
        groups = [(s, fc) for s in range(self.nseq) for fc in range(8)]
        units = [(gi, t, pp) for gi in range(len(groups)) for t in range(4) for pp in range(4)]
        gbuf = {}
        epc = [0]

        def load_group(gi):
            s_, fc_ = groups[gi]
            ub, tb = ubuf[gi % 2], tabb[gi % 2]
            self.dma("sp", ub.ap, d["U"][fc_ * 128:(fc_ + 1) * 128, s_ * SEQ:(s_ + 1) * SEQ], (), [ub])
            self.dma("sp", tb.ap.rearrange("p a b c -> p (a b c)"), d["TAB"][fc_], (), [tb])
            gbuf[gi] = (ub, tb)

        def ctx(i):
            gi, t, pp = units[i]
            if t == 0 and pp == 0 and gi not in gbuf:
                load_group(gi)
            if t == 1 and pp == 0 and gi + 1 < len(groups) and (gi + 1) not in gbuf:
                load_group(gi + 1)
            s_, fc_ = groups[gi]
            ub, tb = gbuf[gi]
            return dict(s=s_, fc=fc_, t=t, pp=pp, ub=ub, tb=tb, p=fc_ * 4 + pp, w=wk[i % NW],
                        cosT=tb.ap[:, pp, 0, :], sinT=tb.ap[:, pp, 1, :],
                        pre=PS[4 + 2 * (i % 2)], pim=PS[5 + 2 * (i % 2)])

        def scan(cx, zkey, tkey, carry):
            w, p, t, pp = cx["w"], cx["p"], cx["t"], cx["pp"]
            rb = rho.ap[:, p:p + 1].to_broadcast([128, T])
            if t == 0:
                S.add("dve", lambda h, o=w[zkey].ap, a=rb, b_=w[tkey].ap: h.tensor_tensor_scan(
                    out=o, data0=a, data1=b_, initial=0.0, op0=ALU.mult, op1=ALU.add), [rho, w[tkey]], [w[zkey]])
            else:
                cbuf = carry[pp][(t + 1) % 2]
                S.add("dve", lambda h, o=w[zkey].ap, a=rb, b_=w[tkey].ap, i0=cbuf.ap[:, 0:1]: h.tensor_tensor_scan(
                    out=o, data0=a, data1=b_, initial=i0, op0=ALU.mult, op1=ALU.add),
                    [rho, w[tkey], cbuf], [w[zkey]])

        def emit_pair(ia, ib):
            ca_ = ctx(ia) if ia is not None else None
            cb_ = ctx(ib) if ib is not None else None
            if ca_:
                w, tb, ub = ca_["w"], ca_["tb"], ca_["ub"]
                ucols = ub.ap[:, ca_["t"] * T:(ca_["t"] + 1) * T]
                self.mm(ca_["pre"].ap, bre.ap[:, ca_["p"], :], ucols, True, True, [bre, ub], [ca_["pre"]])
                self.mm(ca_["pim"].ap, bim.ap[:, ca_["p"], :], ucols, True, True, [bim, ub], [ca_["pim"]])
            def A(k):
                if not ca_:
                    return
                w, tb = ca_["w"], ca_["tb"]
                pre, pim, cosT, sinT = ca_["pre"], ca_["pim"], ca_["cosT"], ca_["sinT"]
                if k == 0:
                    self.tt("dve", w["t1"].ap, pre.ap, cosT, ALU.mult, [pre, tb], [w["t1"]])
                elif k == 1:
                    self.tt("dve", w["t4"].ap, pre.ap, sinT, ALU.mult, [pre, tb], [w["t4"]])
                elif k == 2:
                    self.tt("dve", w["t2"].ap, pim.ap, sinT, ALU.mult, [pim, tb], [w["t2"]])
                elif k == 3:
                    self.tt("dve", w["t3"].ap, pim.ap, cosT, ALU.mult, [pim, tb], [w["t3"]])
                elif k == 4:
                    self.tt("dve", w["t1"].ap, w["t1"].ap, w["t2"].ap, ALU.add, [w["t1"], w["t2"]], [w["t1"]])
                elif k == 5:
                    self.tt("dve", w["t3"].ap, w["t3"].ap, w["t4"].ap, ALU.subtract, [w["t3"], w["t4"]], [w["t3"]])
            def B(k):
                if not cb_:
                    return
                w, tb, cosT, sinT = cb_["w"], cb_["tb"], cb_["cosT"], cb_["sinT"]
                if k == 0:
                    scan(cb_, "zr", "t1", xr)
                elif k == 1:
                    scan(cb_, "zi", "t3", xi)
                elif k == 2:
                    self.tt("dve", w["q1"].ap, w["zr"].ap, cosT, ALU.mult, [w["zr"], tb], [w["q1"]])
                elif k == 3:
                    self.tt("dve", w["q2"].ap, w["zi"].ap, sinT, ALU.mult, [w["zi"], tb], [w["q2"]])
                elif k == 4:
                    self.tt("dve", w["q3"].ap, w["zr"].ap, sinT, ALU.mult, [w["zr"], tb], [w["q3"]])
                elif k == 5:
                    self.tt("dve", w["q4"].ap, w["zi"].ap, cosT, ALU.mult, [w["zi"], tb], [w["q4"]])
            for k in range(6):
                A(k)
                B(k)
            if not cb_:
                return
            w, tb, cosT, sinT, t, pp, p = cb_["w"], cb_["tb"], cb_["cosT"], cb_["sinT"], cb_["t"], cb_["pp"], cb_["p"]
            ub, fc_, s_ = cb_["ub"], cb_["fc"], cb_["s"]
            cur = t % 2
            if t < 3:
                cT, sT = cosT[:, T - 1:T], sinT[:, T - 1:T]
                zrl, zil = w["zr"].ap[:, T - 1:T], w["zi"].ap[:, T - 1:T]
                self.tt("pool", ca.ap, zrl, cT, ALU.mult, [w["zr"], tb], [ca])
                self.tt("pool", cb.ap, zil, sT, ALU.mult, [w["zi"], tb], [cb])
                self.tt("pool", xr[pp][cur].ap, ca.ap, cb.ap, ALU.subtract, [ca, cb], [xr[pp][cur]])
                self.tt("pool", ca.ap, zrl, sT, ALU.mult, [w["zr"], tb], [ca])
                self.tt("pool", cb.ap, zil, cT, ALU.mult, [w["zi"], tb], [cb])
                self.tt("pool", xi[pp][cur].ap, ca.ap, cb.ap, ALU.add, [ca, cb], [xi[pp][cur]])
            y = PS[t]
            self.mm(y.ap, c1.ap[:, p, :], w["q1"].ap, pp == 0, False, [c1, w["q1"]], [y])
            self.mm(y.ap, c2.ap[:, p, :], w["q2"].ap, False, False, [c2, w["q2"]], [y])
            self.mm(y.ap, c3.ap[:, p, :], w["q3"].ap, False, False, [c3, w["q3"]], [y])
            self.mm(y.ap, c3.ap[:, p, :], w["q4"].ap, False, pp == 3, [c3, w["q4"]], [y])
            if pp == 3:
                yy, ss, zz = yb[epc[0] % 2], sb[epc[0] % 2], zg[epc[0] % 2]
                epc[0] += 1
                self.stt("dve", yy.ap, ub.ap[:, t * T:(t + 1) * T], dcol.ap[:, fc_:fc_ + 1], PS[t].ap, ALU.mult, ALU.add,
                         [ub, dcol, PS[t]], [yy])
                self.act(ss.ap, yy.ap, AF.Square, [yy], [ss])
                self.act(ss.ap, ss.ap, AF.Identity, [ss], [ss], scale=0.044715, bias=self.oneb.ap[:, 0:1])
                self.tt("dve", ss.ap, ss.ap, yy.ap, ALU.mult, [ss, yy], [ss])
                self.act(ss.ap, ss.ap, AF.Sigmoid, [ss], [ss], scale=float(2.0 * math.sqrt(2.0 / math.pi)))
                self.tt("dve", zz.ap, yy.ap, ss.ap, ALU.mult, [yy, ss], [zz])
                tg = s_ * 4 + t
                self.dma("sp", d["Z"][fc_ * 128:(fc_ + 1) * 128, tg * T:(tg + 1) * T], zz.ap, [zz], ())

        nU = len(units)
        emit_pair(0, None)
        for i in range(nU):
            emit_pair(i + 1 if i + 1 < nU else None, i)

    def phase_s5c(self, layer, X):
        A, d = self.A, self.d
        self.phase_begin()
        PS = self.PS
        wg = A.alloc([128, 8, 2 * D], BF16, "wglu")
        self.load_w(wg, d["wglu"][layer], 8)
        xb = [A.alloc([128, 8, T], F32, f"x{i}") for i in range(2)]
        zb = [A.alloc([128, 8, T], BF16, f"z{i}") for i in range(2)]
        sg = [A.alloc([128, T], F32, f"sg{i}") for i in range(2)]
        self.dma("sp", xb[0].ap, self.xview(X, 0), (), [xb[0]])
        self.dma("sp", zb[0].ap, self.xview(d["Z"], 0), (), [zb[0]])
        for t in range(self.ntile):
            xt, zt = xb[t % 2], zb[t % 2]
            if t + 1 < self.ntile:
                self.dma("sp", xb[(t + 1) % 2].ap, self.xview(X, t + 1), (), [xb[(t + 1) % 2]])
                self.dma("sp", zb[(t + 1) % 2].ap, self.xview(d["Z"], t + 1), (), [zb[(t + 1) % 2]])
            for m in range(8):
                v_ps, g_ps = PS[2 * (m % 2)], PS[2 * (m % 2) + 1]
                for k in range(8):
                    self.mm(v_ps.ap, wg.ap[:, k, m * 128:(m + 1) * 128], zt.ap[:, k, :], k == 0, k == 7, [wg, zt], [v_ps])
                for k in range(8):
                    self.mm(g_ps.ap, wg.ap[:, k, D + m * 128:D + (m + 1) * 128], zt.ap[:, k, :], k == 0, k == 7,
                            [wg, zt], [g_ps])
                s = sg[m % 2]
                self.act(s.ap, g_ps.ap, AF.Sigmoid, [g_ps], [s])
                self.tt("dve", s.ap, s.ap, v_ps.ap, ALU.mult, [s, v_ps], [s])
                self.tt("pool", xt.ap[:, m, :], xt.ap[:, m, :], s.ap, ALU.add, [xt, s], [xt])
            self.dma("sp", self.xview(X, t), xt.ap, [xt], ())

    def phase_rope(self):
        A, S, d, c = self.A, self.S, self.d, self.c
        self.phase_begin()
        W = 2048
        pm = A.alloc([32, 1], F32, "pm")
        hi = A.alloc([32, 1], F32, "hi")
        self.ts("dve", hi.ap, c["io_p"].ap[0:32, 0:1], 16.0, 16.0, ALU.is_ge, ALU.mult, [c["io_p"]], [hi])
        self.tt("dve", pm.ap, c["io_p"].ap[0:32, 0:1], hi.ap, ALU.subtract, [c["io_p"], hi], [pm])
        invf = A.alloc([32, 1], F32, "invf")
        self.act(invf.ap, pm.ap, AF.Exp, [pm], [invf], scale=float(-math.log(10000.0) / 16.0))
        sgn = A.alloc([32, 1], F32, "sgn")
        self.ts("dve", sgn.ap, hi.ap, 1.0 / 8.0, -1.0, ALU.mult, ALU.add, [hi], [sgn])
        pi_ = A.alloc([32, W], I32, "pos_i")
        pf = A.alloc([32, W], F32, "pos_f")
        ang = A.alloc([32, W], F32, "ang")
        tf = A.alloc([32, W], F32, "tf")
        ti = A.alloc([32, W], I32, "ti")
        o = A.alloc([32, W], F32, "o")
        for j in range(self.ntok // W):
            self.dma("sp", pi_.ap, d["pos"][j * W:(j + 1) * W].partition_broadcast(32), (), [pi_])
            self.cp("dve", pf.ap, pi_.ap, [pi_], [pf])
            self.ts("dve", ang.ap, pf.ap, invf.ap[:, 0:1], None, ALU.mult, None, [pf, invf], [ang])
            self.range_reduce("dve", ang, tf, ti, None)
            self.act(o.ap, ang.ap, AF.Sin, [ang], [o])
            self.ts("dve", o.ap, o.ap, sgn.ap[:, 0:1], None, ALU.mult, None, [o, sgn], [o])
            self.dma("sp", d["ROPE"][1][:, j * W:(j + 1) * W], o.ap, [o], ())
            self.ts("dve", ang.ap, pf.ap, invf.ap[:, 0:1], float(np.pi / 2), ALU.mult, ALU.add, [pf, invf], [ang])
            self.range_reduce("dve", ang, tf, ti, None)
            self.act(o.ap, ang.ap, AF.Sin, [ang], [o])
            self.dma("sp", d["ROPE"][0][:, j * W:(j + 1) * W], o.ap, [o], ())

    def phase_kv(self, X):
        A, S, d, c = self.A, self.S, self.d, self.c
        self.phase_begin()
        PS = self.PS
        wa = A.alloc([128, 8, 288], BF16, "wkva")
        wasw = A.alloc([128, 8, 32], BF16, "wkvasw")
        wbk = A.alloc([128, 2, 1024], BF16, "wkvbk")
        wbv = A.alloc([128, 2, 1024], BF16, "wkvbv")
        self.load_w(wa, d["wkva"], 8)
        self.load_w(wasw, d["wkvasw"], 8)
        self.load_w(wbk, d["wkvbk"], 2)
        self.load_w(wbv, d["wkvbv"], 2)
        gin = A.alloc([128, 8], F32, "gin")
        ga = A.alloc([128, 2], F32, "ga")
        gk = A.alloc([128, 1], F32, "gk")
        gr = A.alloc([32, 2], F32, "gr")
        self.dma("sp", gin.ap, d["kvinn"], (), [gin])
        self.dma("sp", ga.ap, d["kvan"], (), [ga])
        self.dma("sp", gk.ap, d["kn128"], (), [gk])
        self.dma("sp", gr.ap, d["kr32"], (), [gr])
        xb = [A.alloc([128, 8, T], F32, f"x{i}") for i in range(2)]
        rp = [A.alloc([32, 2, T], F32, f"rp{i}") for i in range(2)]
        h = A.alloc([128, 8, T], BF16, "h")
        sq = A.alloc([128, 8, T], BF16, "sq")
        std = A.alloc([128, T], F32, "std")
        rstd = A.alloc([128, T], F32, "rstd")
        cf = A.alloc([128, 2, T], F32, "cf")
        cn = A.alloc([128, 2, T], BF16, "cn")
        kr = A.alloc([32, T], F32, "kr")
        krs = A.alloc([32, T], F32, "krs")
        kro = A.alloc([32, T], BF16, "kro")
        kn = [A.alloc([128, T], BF16, f"kn{i}") for i in range(2)]
        vb = [A.alloc([128, 1024], BF16, f"v{i}") for i in range(2)]
        rope_v = d["ROPE"].rearrange("a p t -> p a t")
        self.dma("sp", xb[0].ap, self.xview(X, 0), (), [xb[0]])
        self.dma("sp", rp[0].ap, rope_v[:, :, 0:T], (), [rp[0]])
        for t in range(self.ntile):
            xt, rt = xb[t % 2], rp[t % 2]
            if t + 1 < self.ntile:
                self.dma("sp", xb[(t + 1) % 2].ap, self.xview(X, t + 1), (), [xb[(t + 1) % 2]])
                self.dma("sp", rp[(t + 1) % 2].ap, rope_v[:, :, (t + 1) * T:(t + 2) * T], (), [rp[(t + 1) % 2]])
            self.rmsnorm_tile(xt, gin, h, sq, PS[6], std, rstd)
            for m in range(2):
                ps = PS[m]
                for k in range(8):
                    self.mm(ps.ap, wa.ap[:, k, m * 128:(m + 1) * 128], h.ap[:, k, :], k == 0, k == 7, [wa, h], [ps])
                self.cp("dve", cf.ap[:, m, :], ps.ap, [ps], [cf])
                self.act(sq.ap[:, m, :], ps.ap, AF.Square, [ps], [sq])
            self.rms_rstd([(sq.ap[:, m, :], 128) for m in range(2)], c["ones"], 128, PS[6], std, rstd, [sq], 1.0 / 256)
            for m in range(2):
                self.stt("dve", cn.ap[:, m, :], cf.ap[:, m, :], ga.ap[:, m:m + 1], rstd.ap, ALU.mult, ALU.mult,
                         [cf, ga, rstd], [cn])
            p1, p2 = PS[2], PS[3]
            for k in range(8):
                self.mm(p1.ap[0:32, :], wa.ap[:, k, 256:288], h.ap[:, k, :], k == 0, k == 7, [wa, h], [p1])
            for k in range(8):
                self.mm(p2.ap[0:32, :], wasw.ap[:, k, :], h.ap[:, k, :], k == 0, k == 7, [wasw, h], [p2])
            self.act(sq.ap[0:32, 2, :], p1.ap[0:32, :], AF.Square, [p1], [sq])
            self.rms_rstd([(sq.ap[0:32, 2, :], 32)], c["ones"], 32, PS[6], std, rstd, [sq], 1.0 / 32)
            self.stt("dve", kr.ap, p1.ap[0:32, :], gr.ap[:, 0:1], rstd.ap[0:32, :], ALU.mult, ALU.mult, [p1, gr, rstd], [kr])
            self.stt("dve", krs.ap, p2.ap[0:32, :], gr.ap[:, 1:2], rstd.ap[0:32, :], ALU.mult, ALU.mult, [p2, gr, rstd], [krs])
            self.tt("pool", kr.ap, kr.ap, rt.ap[:, 0, :], ALU.mult, [kr, rt], [kr])
            self.tt("pool", krs.ap, krs.ap, rt.ap[:, 1, :], ALU.mult, [krs, rt], [krs])
            self.tt("pool", kro.ap, kr.ap, krs.ap, ALU.add, [kr, krs], [kro])
            for hh in range(16):
                self.dma("sp", d["KT"][hh, 64:96, t * T:(t + 1) * T], kro.ap, [kro], ())
            for hp in range(8):
                ps = PS[hp % 2]
                for k in range(2):
                    self.mm(ps.ap, wbk.ap[:, k, hp * 128:(hp + 1) * 128], cn.ap[:, k, :], k == 0, k == 1, [wbk, cn], [ps])
                self.act(sq.ap[:, 3 + hp % 2, :], ps.ap, AF.Square, [ps], [sq])
                ms = PS[4 + hp % 2]
                self.mm(ms.ap, c["ones64x2"].ap, sq.ap[:, 3 + hp % 2, :], True, True, [c["ones64x2"], sq], [ms])
                self.act(std.ap, ms.ap, AF.Ln, [ms], [std], bias=self.epsb.ap[:, 0:1], scale=1.0 / 64)
                self.act(rstd.ap, std.ap, AF.Exp, [std], [rstd], scale=-0.5)
                kk = kn[hp % 2]
                self.stt("dve", kk.ap, ps.ap, gk.ap[:, 0:1], rstd.ap, ALU.mult, ALU.mult, [ps, gk, rstd], [kk])
                self.dma("sp", d["KT"][2 * hp, 0:64, t * T:(t + 1) * T], kk.ap[0:64, :], [kk], ())
                self.dma("sp", d["KT"][2 * hp + 1, 0:64, t * T:(t + 1) * T], kk.ap[64:128, :], [kk], ())
            for tb_ in range(4):
                vv = vb[tb_ % 2]
                for half in range(2):
                    ps = PS[2 + half]
                    for k in range(2):
                        self.mm(ps.ap, cn.ap[:, k, tb_ * 128:(tb_ + 1) * 128], wbv.ap[:, k, half * 512:(half + 1) * 512],
                                k == 0, k == 1, [cn, wbv], [ps])
                    if half == 0:
                        self.act(vv.ap[:, 0:512], ps.ap, AF.Copy, [ps], [vv])
                    else:
                        self.cp("dve", vv.ap[:, 512:1024], ps.ap, [ps], [vv])
                r0 = t * T + tb_ * 128
                self.dma("sp", d["V"][r0:r0 + 128, :], vv.ap, [vv], ())

    def phase_q(self, j, X):
        A, S, d, c = self.A, self.S, self.d, self.c
        self.phase_begin()
        PS = self.PS
        layer = 2 + j
        wqa = A.alloc([128, 8, 384], BF16, "wqa")
        wqb = A.alloc([128, 3, 1536], BF16, "wqb")
        wqs = A.alloc([128, 3, 512], BF16, "wqbsw")
        self.load_w(wqa, d["wqa"][j], 8)
        self.load_w(wqb, d["wqb"][j], 3)
        self.load_w(wqs, d["wqbsw"][j], 3)
        gm = A.alloc([128, 8], F32, "gm")
        gq = A.alloc([128, 3], F32, "gq")
        g96 = A.alloc([96, 2], F32, "g96")
        self.dma("sp", gm.ap, d["mixn"][layer], (), [gm])
        self.dma("sp", gq.ap, d["qan"][j], (), [gq])
        self.dma("sp", g96.ap, d["qg96"][j], (), [g96])
        xb = [A.alloc([128, 8, T], F32, f"x{i}") for i in range(2)]
        rp = [A.alloc([96, 2, T], F32, f"rp{i}") for i in range(2)]
        h = A.alloc([128, 8, T], BF16, "h")
        sq = A.alloc([128, 8, T], BF16, "sq")
        std = A.alloc([128, T], F32, "std")
        rstd = A.alloc([128, T], F32, "rstd")
        cqf = A.alloc([128, 3, T], F32, "cqf")
        cqn = A.alloc([128, 3, T], BF16, "cqn")
        qT = [A.alloc([96, 16, T], BF16, f"qT{i}") for i in range(2)]
        qr = [A.alloc([96, T], F32, f"qr{i}") for i in range(2)]
        qs = [A.alloc([96, T], F32, f"qs{i}") for i in range(2)]
        sqq = [A.alloc([96, T], BF16, f"sqq{i}") for i in range(2)]
        stq = [A.alloc([96, T], F32, f"stq{i}") for i in range(2)]
        rsq = [A.alloc([96, T], F32, f"rsq{i}") for i in range(2)]
        rope_v = d["ROPE"].rearrange("a p t -> p a t")
        qt_v = d["QT"].rearrange("h p t -> p h t")
        self.dma("sp", xb[0].ap, self.xview(X, 0), (), [xb[0]])
        self.dma("sp", rp[0].ap[64:96], rope_v[:, :, 0:T], (), [rp[0]])
        for t in range(self.ntile):
            xt, rt, qq = xb[t % 2], rp[t % 2], qT[t % 2]
            if t + 1 < self.ntile:
                self.dma("sp", xb[(t + 1) % 2].ap, self.xview(X, t + 1), (), [xb[(t + 1) % 2]])
                self.dma("sp", rp[(t + 1) % 2].ap[64:96], rope_v[:, :, (t + 1) * T:(t + 2) * T], (), [rp[(t + 1) % 2]])
            self.rmsnorm_tile(xt, gm, h, sq, PS[6], std, rstd)
            for m in range(3):
                ps = PS[m]
                for k in range(8):
                    self.mm(ps.ap, wqa.ap[:, k, m * 128:(m + 1) * 128], h.ap[:, k, :], k == 0, k == 7, [wqa, h], [ps])
                self.cp("dve", cqf.ap[:, m, :], ps.ap, [ps], [cqf])
                self.act(sq.ap[:, m, :], ps.ap, AF.Square, [ps], [sq])
            self.rms_rstd([(sq.ap[:, m, :], 128) for m in range(3)], c["ones"], 128, PS[6], std, rstd, [sq], 1.0 / 384)
            for m in range(3):
                self.stt("dve", cqn.ap[:, m, :], cqf.ap[:, m, :], gq.ap[:, m:m + 1], rstd.ap, ALU.mult, ALU.mult,
                         [cqf, gq, rstd], [cqn])
            def q_front(hh):
                e = hh % 2
                qp, qsw, ms = PS[e * 3], PS[e * 3 + 1], PS[e * 3 + 2]
                for k in range(3):
                    self.mm(qp.ap[0:96, :], wqb.ap[:, k, hh * 96:(hh + 1) * 96], cqn.ap[:, k, :], k == 0, k == 2, [wqb, cqn], [qp])
                for k in range(3):
                    self.mm(qsw.ap[64:96, :], wqs.ap[:, k, hh * 32:(hh + 1) * 32], cqn.ap[:, k, :], k == 0, k == 2,
                            [wqs, cqn], [qsw])
                sq_, st_, rs_ = sqq[e], stq[e], rsq[e]
                self.act(sq_.ap, qp.ap[0:96, :], AF.Square, [qp], [sq_])
                self.mm(ms.ap[0:96, :], c["ones64x2"].ap[0:96, 0:96], sq_.ap, True, True, [c["ones64x2"], sq_], [ms])
                self.act(st_.ap, ms.ap[0:96, :], AF.Ln, [ms], [st_], bias=self.epsb.ap[0:96, 0:1],
                         scale=c["qscale"].ap[0:96, 0:1])

            def q_back(hh):
                e = hh % 2
                qp, qsw = PS[e * 3], PS[e * 3 + 1]
                rs_, qr_, qs_ = rsq[e], qr[e], qs[e]
                self.act(rs_.ap, stq[e].ap, AF.Exp, [stq[e]], [rs_], scale=-0.5)
                self.stt("dve", qq.ap[0:64, hh, :], qp.ap[0:64, :], g96.ap[0:64, 0:1], rs_.ap[0:64, :], ALU.mult, ALU.mult,
                         [qp, g96, rs_], [qq])
                self.stt("dve", qr_.ap[64:96, :], qp.ap[64:96, :], g96.ap[64:96, 0:1], rs_.ap[64:96, :], ALU.mult, ALU.mult,
                         [qp, g96, rs_], [qr_])
                self.stt("dve", qs_.ap[64:96, :], qsw.ap[64:96, :], g96.ap[64:96, 1:2], rs_.ap[64:96, :], ALU.mult, ALU.mult,
                         [qsw, g96, rs_], [qs_])
                self.tt("pool", qr_.ap[64:96, :], qr_.ap[64:96, :], rt.ap[64:96, 0, :], ALU.mult, [qr_, rt], [qr_])
                self.tt("pool", qs_.ap[64:96, :], qs_.ap[64:96, :], rt.ap[64:96, 1, :], ALU.mult, [qs_, rt], [qs_])
                self.tt("pool", qq.ap[64:96, hh, :], qr_.ap[64:96, :], qs_.ap[64:96, :], ALU.add, [qr_, qs_], [qq])

            q_front(0)
            for hh in range(16):
                if hh + 1 < 16:
                    q_front(hh + 1)
                q_back(hh)
            self.dma("sp", qt_v[:, :, t * T:(t + 1) * T], qq.ap, [qq], ())

    def phase_att(self, j, X):
        A, S, d, c = self.A, self.S, self.d, self.c
        self.phase_begin()
        PS = self.PS
        wo = A.alloc([128, 8, D], BF16, "wo")
        self.load_w(wo, d["wo"][j], 8)
        ktb = A.alloc([96, 16, SEQ], BF16, "kt")
        vbuf = A.alloc([128, 16, 1024], BF16, "v")
        onesb = A.alloc([128, 64], BF16, "ones64")
        self.memset("pool", onesb.ap, 1.0, [onesb])
        xt = A.alloc([128, 8, T], F32, "x")
        qT = [A.alloc([96, 16, T], BF16, f"qT{i}") for i in range(2)]
        NP = 4
        pT = [A.alloc([128, T], BF16, f"pT{i}") for i in range(NP)]
        oT = A.alloc([128, 8, T], BF16, "oT")
        rden = [A.alloc([128, T], F32, f"rden{i}") for i in range(2)]
        qt_v = d["QT"].rearrange("h p t -> p h t")
        scale = 1.0 / math.sqrt(96.0)
        pcount = 0
        self.dma("sp", qT[0].ap, qt_v[:, :, 0:T], (), [qT[0]])
        for t in range(self.ntile):
            s, tq = t // 4, t % 4
            if tq == 0:
                self.dma("sp", ktb.ap, d["KT"].rearrange("h p t -> p h t")[:, :, s * SEQ:(s + 1) * SEQ], (), [ktb])
                self.dma("sp", vbuf.ap, d["V"][s * SEQ:(s + 1) * SEQ, :].rearrange("(k p) f -> p k f", p=128), (), [vbuf])
            qq = qT[t % 2]
            if t + 1 < self.ntile:
                self.dma("sp", qT[(t + 1) % 2].ap, qt_v[:, :, (t + 1) * T:(t + 2) * T], (), [qT[(t + 1) % 2]])
            self.dma("sp", xt.ap, self.xview(X, t), (), [xt])
            nkt = 4 * tq + 4
            units = [(hp, hl, kt) for hp in range(8) for hl in range(2) for kt in range(nkt)]
            LA = 2
            info = {}
            for i in range(len(units) + LA):
                if i < len(units):
                    hp, hl, kt = units[i]
                    hh = 2 * hp + hl
                    i_ = kt - 4 * tq
                    c0 = max(i_, 0) * 128
                    s_ps = PS[pcount % 4]
                    pt = pT[pcount % NP]
                    pcount += 1
                    self.mm(s_ps.ap[:, c0:T], ktb.ap[:, hh, kt * 128:(kt + 1) * 128], qq.ap[:, hh, c0:T], True, True,
                            [ktb, qq], [s_ps])
                    self.act(pt.ap[:, c0:T], s_ps.ap[:, c0:T], AF.Exp, [s_ps], [pt], scale=scale)
                    if i_ >= 0:
                        self.tt("pool", pt.ap[:, c0:c0 + 128], pt.ap[:, c0:c0 + 128], c["mask01"].ap, ALU.mult,
                                [pt, c["mask01"]], [pt])
                    info[i] = (pt, c0)
                j_ = i - LA
                if j_ >= 0:
                    hp, hl, kt = units[j_]
                    hh = 2 * hp + hl
                    pt, c0 = info.pop(j_)
                    rows = slice(64 * hl, 64 * hl + 64)
                    o_ps, d_ps = PS[4 + 2 * (hp % 2)], PS[5 + 2 * (hp % 2)]
                    self.mm(o_ps.ap[rows, c0:T], vbuf.ap[:, kt, hh * 64:(hh + 1) * 64], pt.ap[:, c0:T],
                            kt == 0, kt == nkt - 1, [vbuf, pt], [o_ps])
                    self.mm(d_ps.ap[rows, c0:T], onesb.ap, pt.ap[:, c0:T], kt == 0, kt == nkt - 1, [onesb, pt], [d_ps])
                    if hl == 1 and kt == nkt - 1:
                        rd = rden[hp % 2]
                        self.recip(rd.ap, d_ps.ap, [d_ps], [rd])
                        self.tt("dve", oT.ap[:, hp, :], o_ps.ap, rd.ap, ALU.mult, [o_ps, rd], [oT])
            for m in range(8):
                ps = PS[m % 4]
                for k in range(8):
                    self.mm(ps.ap, wo.ap[:, k, m * 128:(m + 1) * 128], oT.ap[:, k, :], k == 0, k == 7, [wo, oT], [ps])
                self.tt("dve", xt.ap[:, m, :], xt.ap[:, m, :], ps.ap, ALU.add, [xt, ps], [xt])
            self.dma("sp", self.xview(X, t), xt.ap, [xt], ())

    def build(self):
        nc = bass.Bass("TRN2", target_bir_lowering=False)
        self.nc = nc
        self.declare(nc)
        d = self.d
        with ExitStack() as es:
            arena_t = es.enter_context(nc.sbuf_tensor("arena", [128, ARENA_BYTES], U8))
            ps_t = es.enter_context(nc.psum_tensor("ps", [128, 8, 512], F32))
            self.S = Sched(nc, es)
            self.A = Arena(arena_t[:, :], ARENA_BYTES)
            self.PS = [Buf(ps_t[:, b, :], f"ps{b}", excl=True) for b in range(8)]
            self.epsb = self.A.alloc([128, 1], F32, "eps")
            self.memset("pool", self.epsb.ap, EPS, [self.epsb])
            self.oneb = self.A.alloc([128, 1], F32, "one")
            self.memset("pool", self.oneb.ap, 1.0, [self.oneb])
            self.consts()
            self.arena_mark = self.A.off
            X = d["X"]
            stages = []
            for l in range(2):
                stages += [("s5a", l), ("s5b", l), ("s5c", l), ("ffn", l)]
            stages += [("rope", 0), ("kv", 0)]
            for j in range(2):
                stages += [("q", j), ("att", j), ("ffn", 2 + j)]
            if self.stop_after is not None:
                stages = stages[:self.stop_after]
            if self.only is not None:
                stages = [stages[i] for i in self.only]
            first = True
            for si, (kind, idx) in enumerate(stages):
                last = si == len(stages) - 1
                if kind == "s5a":
                    self.phase_s5a(idx, d["xT"] if first else X)
                elif kind == "s5b":
                    self.phase_s5b(idx)
                elif kind == "s5c":
                    self.phase_s5c_src = d["xT"] if idx == 0 else X
                    self.phase_s5c2(idx, self.phase_s5c_src, d["outT"] if last else X)
                elif kind == "ffn":
                    self.phase_ffn(idx, X, d["outT"] if last else X)
                elif kind == "rope":
                    self.phase_rope()
                elif kind == "kv":
                    self.phase_kv(X)
                elif kind == "q":
                    self.phase_q(idx, X)
                elif kind == "att":
                    self.phase_att(idx, X)
                first = False
            if stages[-1][0] not in ("ffn", "s5c"):
                self.S.barrier()
                for tt_ in range(self.ntile):
                    self.dma("sp", self.xview(d["outT"], tt_), self.xview(X, tt_), (), ())
            self.S.finish()
            block = es.enter_context(nc.Block())
            self.S.emit(block)
        return nc

    def phase_s5c2(self, layer, src, dst):
        A, d = self.A, self.d
        self.phase_begin()
        PS = self.PS
        wg = A.alloc([128, 8, 2 * D], BF16, "wglu")
        self.load_w(wg, d["wglu"][layer], 8)
        xb = [A.alloc([128, 8, T], F32, f"x{i}") for i in range(2)]
        zb = [A.alloc([128, 8, T], BF16, f"z{i}") for i in range(2)]
        sg = [A.alloc([128, T], F32, f"sg{i}") for i in range(2)]
        self.dma("sp", xb[0].ap, self.xview(src, 0), (), [xb[0]])
        self.dma("sp", zb[0].ap, self.xview(d["Z"], 0), (), [zb[0]])
        for t in range(self.ntile):
            xt, zt = xb[t % 2], zb[t % 2]
            if t + 1 < self.ntile:
                self.dma("sp", xb[(t + 1) % 2].ap, self.xview(src, t + 1), (), [xb[(t + 1) % 2]])
                self.dma("sp", zb[(t + 1) % 2].ap, self.xview(d["Z"], t + 1), (), [zb[(t + 1) % 2]])
            for m in range(8):
                v_ps, g_ps = PS[2 * (m % 2)], PS[2 * (m % 2) + 1]
                for k in range(8):
                    self.mm(v_ps.ap, wg.ap[:, k, m * 128:(m + 1) * 128], zt.ap[:, k, :], k == 0, k == 7, [wg, zt], [v_ps])
                for k in range(8):
                    self.mm(g_ps.ap, wg.ap[:, k, D + m * 128:D + (m + 1) * 128], zt.ap[:, k, :], k == 0, k == 7,
                            [wg, zt], [g_ps])
                s = sg[m % 2]
                self.act(s.ap, g_ps.ap, AF.Sigmoid, [g_ps], [s])
                self.tt("dve", s.ap, s.ap, v_ps.ap, ALU.mult, [s, v_ps], [s])
                self.tt("pool", xt.ap[:, m, :], xt.ap[:, m, :], s.ap, ALU.add, [xt, s], [xt])
            self.dma("sp", self.xview(dst, t), xt.ap, [xt], ())


def _pc(v, nchunk):
    v = np.asarray(v, np.float32)
    return np.ascontiguousarray(v.reshape(v.shape[:-1] + (nchunk, 128)).swapaxes(-1, -2))


def host_layout(inp, nseq, core):
    f = lambda a: np.ascontiguousarray(np.asarray(a, np.float32))
    x = np.asarray(inp["x"], np.float32)[core * nseq:(core + 1) * nseq]
    m = {}
    m["xT"] = np.ascontiguousarray(x.reshape(nseq * SEQ, D).T)
    m["pos"] = np.ascontiguousarray(np.asarray(inp["positions"], np.int32)[core * nseq:(core + 1) * nseq].reshape(-1))
    return m


def host_shared(inp):
    f = lambda a: np.ascontiguousarray(np.asarray(a, np.float32))
    m = {}
    m["mixn"] = _pc(inp["mix_norm"], 8)
    m["ffnn"] = _pc(inp["ffn_norm"], 8)
    m["wgu"] = f(inp["ffn_w_gate_up"])
    m["wdn"] = f(inp["ffn_w_down"])
    m["win"] = f(inp["ssm_w_in"])
    lr, li, ls = f(inp["ssm_lambda_re"]), f(inp["ssm_lambda_im"]), f(inp["ssm_log_step"])
    lsb = np.broadcast_to(ls[:, :, None], lr.shape)
    def colrow(a):
        a = a.reshape(2, 32, 128)
        return a
    cols = np.stack([colrow(lr), colrow(li), colrow(lsb)], axis=1)
    m["s5col"] = np.ascontiguousarray(cols.transpose(0, 3, 1, 2))
    m["s5row"] = np.ascontiguousarray(cols.reshape(2, 3, 4096))
    bre, bim = f(inp["ssm_b_re"]), f(inp["ssm_b_im"])
    cre, cim = f(inp["ssm_c_re"]), f(inp["ssm_c_im"])
    def pad_b(b):
        out = np.zeros((2, 128, 32, 128), np.float32)
        for g in range(64):
            p, gp = g // 2, g % 2
            gl = g % 8
            out[:, gl * 16:(gl + 1) * 16, p, gp * 64:(gp + 1) * 64] = b[:, g].transpose(0, 2, 1)
        return out
    def pad_c(cc):
        out = np.zeros((2, 128, 32, 128), np.float32)
        for g in range(64):
            p, gp = g // 2, g % 2
            gl = g % 8
            out[:, gp * 64:(gp + 1) * 64, p, gl * 16:(gl + 1) * 16] = cc[:, g].transpose(0, 2, 1)
        return out
    m["lbre"], m["lbim"] = pad_b(bre), pad_b(bim)
    m["gre"], m["gim"] = pad_c(cre), pad_c(cim)
    m["dsk"] = _pc(inp["ssm_d"], 8)
    m["wglu"] = f(inp["ssm_w_glu"])
    m["kvinn"] = _pc(inp["kv_in_norm"], 8)
    wkva = f(inp["mla_w_kv_a"])
    m["wkva"] = wkva
    m["wkvasw"] = np.ascontiguousarray(np.concatenate([wkva[:, 272:288], wkva[:, 256:272]], axis=1))
    m["kvan"] = _pc(inp["mla_kv_a_norm"], 2)
    wkvb = f(inp["mla_w_kv_b"]).reshape(256, 16, 128)
    m["wkvbk"] = np.ascontiguousarray(wkvb[:, :, 0:64].reshape(256, 1024))
    m["wkvbv"] = np.ascontiguousarray(wkvb[:, :, 64:128].reshape(256, 1024))
    kn = f(inp["mla_k_nope_norm"])
    m["kn128"] = np.ascontiguousarray(np.concatenate([kn, kn]).reshape(128, 1))
    krn = f(inp["mla_k_rope_norm"])
    m["kr32"] = np.ascontiguousarray(np.stack([krn, np.concatenate([krn[16:], krn[:16]])], axis=1))
    m["wqa"] = f(inp["mla_w_q_a"])
    m["qan"] = _pc(inp["mla_q_a_norm"], 3)
    wqb = f(inp["mla_w_q_b"])
    m["wqb"] = wqb
    w4 = wqb.reshape(2, 384, 16, 96)
    m["wqbsw"] = np.ascontiguousarray(np.concatenate([w4[..., 80:96], w4[..., 64:80]], axis=-1).reshape(2, 384, 512))
    qn, qrn = f(inp["mla_q_nope_norm"]), f(inp["mla_q_rope_norm"])
    g0 = np.concatenate([qn, qrn], axis=1)
    g1 = np.concatenate([qn, qrn[:, 16:], qrn[:, :16]], axis=1)
    m["qg96"] = np.ascontiguousarray(np.stack([g0, g1], axis=2))
    m["wo"] = f(inp["mla_w_o"])
    return m


_CACHE = {}


def run(inputs, nseq_per_core, ncore, stop_after=None):
    key = (nseq_per_core, stop_after)
    if key not in _CACHE:
        _CACHE[key] = K(nseq_per_core, stop_after).build()
    nc = _CACHE[key]
    shared = host_shared(inputs)
    in_maps = []
    for c in range(ncore):
        mm_ = dict(shared)
        mm_.update(host_layout(inputs, nseq_per_core, c))
        in_maps.append(mm_)
    res = run_bass_kernel_spmd(nc, in_maps, core_ids=list(range(ncore)))
    outs = [np.asarray(r["outT"]).T.reshape(nseq_per_core, SEQ, D) for r in res.results]
    return np.ascontiguousarray(np.concatenate(outs, axis=0).astype(np.float32))


def kernel(**inputs):
    return run(inputs, 4, NCORE)
```
